# Optimizing a Trainium2 kernel written in Bass

```python
import math
import jax, jax.numpy as jnp
from jax import lax
import numpy as np

D_MODEL = 2048
BATCH = 2
SEQ = 16384
DEPTH = 1

MEM_LEN = 256
D_ATTN = D_MODEL // 2
D_POOL = D_MODEL - D_ATTN
DIFF_HEAD_DIM = 128
N_DIFF_HEADS = D_ATTN // (2 * DIFF_HEAD_DIM)
D_IN = 3 * D_ATTN + D_POOL
POOL_WINDOWS = (2, 4, 8, 16)
N_POOL_GROUPS = len(POOL_WINDOWS)
POOL_GROUP_DIM = D_POOL // N_POOL_GROUPS
N_XATTN_HEADS = 4
XATTN_HEAD_DIM = D_MODEL // N_XATTN_HEADS
D_FF = -(-8 * D_MODEL // (3 * 256)) * 256
Q_BLOCK = 128
EPS = 1e-6
NEG_INF = -1e30

kernel_name = "hymba_diffattn_pool_hybrid"


def _lambda_init(layer_idx):
    return 0.8 - 0.6 * math.exp(-0.3 * layer_idx)


def _rmsnorm(x, g):
    x32 = x.astype(jnp.float32)
    y = x32 * lax.rsqrt(jnp.mean(x32 * x32, axis=-1, keepdims=True) + EPS)
    return (y * g.astype(jnp.float32)).astype(x.dtype)


def _diff_attention(q, k, v, lam):
    B, S, H, _, d = q.shape
    nb = S // Q_BLOCK
    qh = q.transpose(0, 2, 3, 1, 4)
    kh = k.transpose(0, 2, 3, 1, 4)
    vh = v.transpose(0, 2, 1, 3)
    qb = qh.reshape(B, H, 2, nb, Q_BLOCK, d).transpose(3, 0, 1, 2, 4, 5)
    kpos = jnp.arange(S)
    scale = d ** -0.5

    def block(args):
        i, qi = args
        qpos = i * Q_BLOCK + jnp.arange(Q_BLOCK)
        s = jnp.einsum('bhmqd,bhmkd->bhmqk', qi, kh).astype(jnp.float32) * scale
        s = jnp.where(kpos[None, :] <= qpos[:, None], s, NEG_INF)
        p = jax.nn.softmax(s, axis=-1)
        wgt = p[:, :, 0] - lam * p[:, :, 1]
        return jnp.einsum('bhqk,bhkd->bhqd', wgt.astype(vh.dtype), vh)

    o = lax.map(block, (jnp.arange(nb), qb))
    return o.transpose(1, 0, 3, 2, 4).reshape(B, S, H, 2 * d)


def _multiscale_pool(u, w, scale):
    B, S, C = u.shape
    u32 = u.astype(jnp.float32)
    cs = jnp.concatenate([jnp.zeros((B, 1, C), jnp.float32), jnp.cumsum(u32, axis=1)], axis=1)
    csg = cs.reshape(B, S + 1, N_POOL_GROUPS, POOL_GROUP_DIM)
    ug = u32.reshape(B, S, N_POOL_GROUPS, POOL_GROUP_DIM)
    t = jnp.arange(S)
    outs = []
    for g, wl in enumerate(POOL_WINDOWS):
        upper = csg[:, 1:, g]
        lower = jnp.concatenate([jnp.zeros((B, wl - 1, POOL_GROUP_DIM), jnp.float32),
                                 csg[:, :S - wl + 1, g]], axis=1)
        cnt = jnp.minimum(t + 1, wl).astype(jnp.float32)[None, :, None]
        outs.append((upper - lower) / cnt - ug[:, :, g])
    pooled = jnp.stack(outs, axis=2).astype(u.dtype)
    mixed = jnp.einsum('bsgc,gcd->bsgd', pooled, w)
    return mixed.reshape(B, S, C) * scale


def _cross_attention(h, m, wq, wkv, wo):
    B, S, _ = h.shape
    M = m.shape[1]
    q = (h @ wq).reshape(B, S, N_XATTN_HEADS, XATTN_HEAD_DIM)
    kv = (m @ wkv).reshape(B, M, 2, N_XATTN_HEADS, XATTN_HEAD_DIM)
    k, v = kv[:, :, 0], kv[:, :, 1]
    s = jnp.einsum('bshd,bmhd->bhsm', q, k).astype(jnp.float32) * (XATTN_HEAD_DIM ** -0.5)
    p = jax.nn.softmax(s, axis=-1).astype(v.dtype)
    o = jnp.einsum('bhsm,bmhd->bshd', p, v).reshape(B, S, D_MODEL)
    return o @ wo


def _swiglu(h, w_gate, w_up, w_down):
    return (jax.nn.silu(h @ w_gate) * (h @ w_up)) @ w_down


def setup_inputs(seed: int = 0) -> dict:
    key = jax.random.key(seed)
    ks = jax.random.split(key, 24)

    def nrm(k, shape, s):
        return jax.random.normal(k, shape, jnp.float32) * s

    def gain(k, shape):
        return 1.0 + 0.02 * jax.random.normal(k, shape, jnp.float32)

    return {
        "x": nrm(ks[0], (BATCH, SEQ, D_MODEL), 1.0),
        "mem": nrm(ks[1], (BATCH, MEM_LEN, D_MODEL), 1.0),
        "norm_mix": gain(ks[2], (DEPTH, D_MODEL)),
        "w_in": nrm(ks[3], (DEPTH, D_MODEL, D_IN), D_MODEL ** -0.5),
        "lambda_q1": nrm(ks[4], (DEPTH, DIFF_HEAD_DIM), 0.1),
        "lambda_k1": nrm(ks[5], (DEPTH, DIFF_HEAD_DIM), 0.1),
        "lambda_q2": nrm(ks[6], (DEPTH, DIFF_HEAD_DIM), 0.1),
        "lambda_k2": nrm(ks[7], (DEPTH, DIFF_HEAD_DIM), 0.1),
        "subln": gain(ks[8], (DEPTH, 2 * DIFF_HEAD_DIM)),
        "pool_w": nrm(ks[9], (DEPTH, N_POOL_GROUPS, POOL_GROUP_DIM, POOL_GROUP_DIM), POOL_GROUP_DIM ** -0.5),
        "pool_scale": gain(ks[10], (DEPTH, D_POOL)),
        "w_o": nrm(ks[11], (DEPTH, D_MODEL, D_MODEL), D_MODEL ** -0.5),
        "norm_xattn": gain(ks[12], (DEPTH, D_MODEL)),
        "norm_mem": gain(ks[13], (DEPTH, D_MODEL)),
        "wq_x": nrm(ks[14], (DEPTH, D_MODEL, D_MODEL), D_MODEL ** -0.5),
        "wkv_x": nrm(ks[15], (DEPTH, D_MODEL, 2 * D_MODEL), D_MODEL ** -0.5),
        "wo_x": nrm(ks[16], (DEPTH, D_MODEL, D_MODEL), D_MODEL ** -0.5),
        "norm_ffn": gain(ks[17], (DEPTH, D_MODEL)),
        "w_gate": nrm(ks[18], (DEPTH, D_MODEL, D_FF), D_MODEL ** -0.5),
        "w_up": nrm(ks[19], (DEPTH, D_MODEL, D_FF), D_MODEL ** -0.5),
        "w_down": nrm(ks[20], (DEPTH, D_FF, D_MODEL), D_FF ** -0.5),
        "norm_final": gain(ks[21], (D_MODEL,)),
    }


def reference(x, mem, norm_mix, w_in, lambda_q1, lambda_k1, lambda_q2, lambda_k2,
              subln, pool_w, pool_scale, w_o, norm_xattn, norm_mem, wq_x, wkv_x,
              wo_x, norm_ffn, w_gate, w_up, w_down, norm_final):
    B, S, _ = x.shape
    for l in range(DEPTH):
        lam_init = _lambda_init(l)
        h = _rmsnorm(x, norm_mix[l])
        proj = h @ w_in[l]
        q = proj[..., :D_ATTN].reshape(B, S, N_DIFF_HEADS, 2, DIFF_HEAD_DIM)
        k = proj[..., D_ATTN:2 * D_ATTN].reshape(B, S, N_DIFF_HEADS, 2, DIFF_HEAD_DIM)
        v = proj[..., 2 * D_ATTN:3 * D_ATTN].reshape(B, S, N_DIFF_HEADS, 2 * DIFF_HEAD_DIM)
        u = proj[..., 3 * D_ATTN:]
        lam = (jnp.exp(jnp.sum(lambda_q1[l].astype(jnp.float32) * lambda_k1[l].astype(jnp.float32)))
               - jnp.exp(jnp.sum(lambda_q2[l].astype(jnp.float32) * lambda_k2[l].astype(jnp.float32)))
               + lam_init)
        o_attn = _diff_attention(q, k, v, lam)
        o_attn = (_rmsnorm(o_attn, subln[l]) * (1.0 - lam_init)).reshape(B, S, D_ATTN)
        o_pool = _multiscale_pool(u, pool_w[l], pool_scale[l])
        x = x + jnp.concatenate([o_attn, o_pool], axis=-1) @ w_o[l]
        hx = _rmsnorm(x, norm_xattn[l])
        m = _rmsnorm(mem, norm_mem[l])
        x = x + _cross_attention(hx, m, wq_x[l], wkv_x[l], wo_x[l])
        hf = _rmsnorm(x, norm_ffn[l])
        x = x + _swiglu(hf, w_gate[l], w_up[l], w_down[l])
    return _rmsnorm(x, norm_final)
```

```python
import math
from contextlib import ExitStack
import numpy as np
import concourse.bass as bass
import concourse.mybir as mybir
from concourse.bass_utils import run_bass_kernel_spmd

F32 = mybir.dt.float32
BF16 = mybir.dt.bfloat16
AF = mybir.ActivationFunctionType
ALU = mybir.AluOpType

D = 2048
DFF = 5632
MEM = 256
EPS = 1e-6
LAM_INIT = 0.8 - 0.6 * math.exp(0.0)
NEG = -30000.0
NW = 3
ARENA_KB = 207


class _Stop(Exception):
    pass


STOP_AT = 99


class Sem:
    def __init__(self, h):
        self.h = h
        self.n = 0


class Buf:
    def __init__(self):
        self.w = []
        self.r = []

    def wdeps(self):
        return self.w + self.r

    def rdeps(self):
        return list(self.w)

    def wrote(self, *toks):
        self.w = [t for t in toks if t is not None]
        self.r = []

    def read(self, *toks):
        self.r += [t for t in toks if t is not None]


class Rec:
    def __init__(self):
        self.q = {e: [] for e in ("pe", "act", "dve", "pool", "sp")}
        self.waited = {}
        self.P = {}

    def wait(self, eng, tok):
        if tok is None:
            return
        sem, val = tok
        key = (eng, id(sem))
        if self.waited.get(key, 0) >= val:
            return
        self.waited[key] = val
        self.q[eng].append(lambda e, h=sem.h, v=val: e.wait_ge(h, v))

    def op(self, eng, fn, waits=(), sem=None, inc=1, mark=False):
        for w in waits:
            self.wait(eng, w)
        if mark and sem is None:
            sem = self.P[eng]
        if sem is None:
            self.q[eng].append(lambda e, fn=fn: fn(e))
            return None
        sem.n += inc
        val = sem.n
        self.q[eng].append(lambda e, fn=fn, h=sem.h, k=inc: fn(e).then_inc(h, k))
        return (sem, val)


class Arena:
    def __init__(self, ap):
        self.ap = ap
        self.off = 0
        self.cap = ap.shape[1] * 2

    def alloc(self, nelem, dt=BF16):
        nb = nelem * (4 if dt is F32 else 2)
        nb = (nb + 63) // 64 * 64
        assert self.off + nb <= self.cap, ("arena overflow", self.off, nb, self.cap)
        v = self.ap[:, self.off // 2:(self.off + nb) // 2]
        self.off += nb
        if dt is F32:
            v = v.bitcast(F32)
        return v[:, 0:nelem]


def build_program(S):
    NT = S // 128
    SL = S // 4
    G = min(512, SL)
    TG = G // 128
    NG = SL // G
    TS = min(2, NT)
    nc = bass.Bass("TRN2", target_bir_lowering=False)

    def din(name, shape, dt=F32):
        return nc.dram_tensor(name, list(shape), dt, kind="ExternalInput")

    x_sl = din("x_sl", [SL, D])
    mem_in = din("mem_b", [MEM, D])
    w_in = din("w_in_c", [D, 1024])
    gT_mix = din("gT_mix", [128, 16])
    g_xat = din("g_xat", [D])
    g_mem = din("g_mem", [D])
    g_ffn = din("g_ffn", [D])
    g_fin = din("g_fin", [D])
    lam_in = din("lam_in", [4 * 128])
    subln_in = din("subln_in", [256])
    poolw_in = din("poolw_in", [256, 256])
    psc_in = din("psc_in", [128, 2])
    bands_in = din("bands_in", [128, 3 * 128])
    ident_in = din("ident_in", [128, 128])
    mask_in = din("mask_in", [128, 128])
    x_full = din("x_full", [S, D])
    w_o_in = din("w_o_p", [D, D])
    wq_in = din("wq_x", [D, D])
    wkv_in = din("wkv_x", [D, 2 * D])
    wo_in = din("wo_x", [D, D])
    wg_in = din("w_gate", [D, DFF])
    wu_in = din("w_up", [D, DFF])
    wd_in = din("w_down", [DFF, D])
    y_out = nc.dram_tensor("y", [SL, D], F32, kind="ExternalOutput")

    PIECE = min(1024, SL)
    NP = S // PIECE
    xch_in = [nc.dram_tensor(f"xch_in{j}", [512, PIECE], BF16) for j in range(NP)]
    xch_out = nc.dram_tensor("xch_out", [NP, 2048, PIECE], BF16)
    KD = DFF // 128
    s_wo = nc.dram_tensor("s_wo", [4, 128, 16 * 512], BF16)
    s_wq = nc.dram_tensor("s_wq", [4, 128, 16 * 512], BF16)
    s_wox = nc.dram_tensor("s_wox", [4, 128, 16 * 512], BF16)
    s_wkv = nc.dram_tensor("s_wkv", [8, 128, 16 * 512], BF16)
    s_wg = nc.dram_tensor("s_wg", [11, 128, 16 * 512], BF16)
    s_wu = nc.dram_tensor("s_wu", [11, 128, 16 * 512], BF16)
    s_wd = nc.dram_tensor("s_wd", [4, 128, KD * 512], BF16)

    R = Rec()

    with ExitStack() as es:
        arena_t = es.enter_context(nc.sbuf_tensor("arena", [128, ARENA_KB * 512], BF16))
        ps = es.enter_context(nc.psum_tensor("psum", [128, 4096], F32))

        def mksem(name):
            return es.enter_context(nc.semaphore(name))

        h_pe, h_act, h_dve, h_pool = mksem("p_pe"), mksem("p_act"), mksem("p_dve"), mksem("p_pool")
        h_x0, h_x1, h_set, h_wc = mksem("s_x0"), mksem("s_x1"), mksem("s_set"), mksem("s_wc")
        h_setp = mksem("s_setp")
        h_gc, h_gx = mksem("s_gc"), mksem("s_gx")
        h_st0, h_st1, h_cc = mksem("s_st0"), mksem("s_st1"), mksem("s_cc")
        h_r0, h_r1, h_r2 = mksem("s_r0"), mksem("s_r1"), mksem("s_r2")
        h_xr, h_ot, h_y0, h_y1, h_mem = mksem("s_xr"), mksem("s_ot"), mksem("s_y0"), mksem("s_y1"), mksem("s_mem")
        h_y2, h_y3 = mksem("s_y2"), mksem("s_y3")
        block = es.enter_context(nc.Block())
        R.P = {"pe": Sem(h_pe), "act": Sem(h_act), "dve": Sem(h_dve), "pool": Sem(h_pool)}
        SX = [Sem(h_x0), Sem(h_x1)]
        SSET = Sem(h_set)
        SSETP = Sem(h_setp)
        SGC = Sem(h_gc)
        SGX = Sem(h_gx)
        SWC = Sem(h_wc)
        SST = [Sem(h_st0), Sem(h_st1)]
        SCC = Sem(h_cc)
        SR = [Sem(h_r0), Sem(h_r1), Sem(h_r2)]
        SXR = Sem(h_xr)
        SOT = Sem(h_ot)
        SY = [Sem(h_y0), Sem(h_y1), Sem(h_y2), Sem(h_y3)]
        SMEM = Sem(h_mem)

        A = Arena(arena_t[:])

        def bank(b, n=1):
            return ps[:, b * 512:(b + n) * 512]

        def bank16(b, n=1):
            return bank(b, n).bitcast(BF16)

        BK = [Buf() for _ in range(8)]
        B_junk = Buf()
        B_junkd = Buf()

        idb = A.alloc(128)
        onesb = A.alloc(128)
        small = A.alloc(64, F32)
        neghalf = small[:, 0:1]
        neglam = small[:, 1:2]
        dots = small[:, 2:4]
        ee = small[:, 4:6]
        tmp1 = small[:, 6:7]
        junk = A.alloc(2048)
        junkd = A.alloc(256)
        common_end = A.off

        Wb = A.alloc(16 * 1024)
        kT = A.alloc(2 * S)
        Vx = A.alloc(NT * 258)
        xin = [A.alloc(2048, F32), A.alloc(2048, F32)]
        hs = A.alloc(2048)
        hT = A.alloc(2048)
        qk_tm = A.alloc(512)
        qT = [A.alloc(256), A.alloc(256)]
        ut = [A.alloc(256) for _ in range(3)]
        PT = [A.alloc(512) for _ in range(2)]
        maskb = A.alloc(128)
        bandsb = A.alloc(3 * 128)
        pw = A.alloc(2 * 256)
        psc = A.alloc(2, F32)
        gTm = A.alloc(16, F32)
        sub_bc = A.alloc(256, F32)
        lq = A.alloc(512, F32)
        pooledT = A.alloc(256)
        stage = [A.alloc(4 * TS * 128), A.alloc(4 * TS * 128)]
        o1 = lq[:, 0:256]
        osb = lq[:, 256:512]
        ob = A.alloc(256)
        ssq = A.alloc(NT, F32)
        rstd = A.alloc(NT, F32)
        fin = A.alloc(8, F32)
        kT3 = kT.rearrange("p (h s) -> p h s", h=2)
        Vx3 = Vx.rearrange("p (t c) -> p t c", c=258)
        Wb3 = Wb.rearrange("p (c n) -> p c n", n=1024)

        B_xin = [Buf(), Buf()]
        B_hs = Buf()
        B_hT = Buf()
        B_qk = Buf()
        B_qT = [Buf(), Buf()]
        B_ut = [Buf(), Buf(), Buf()]
        B_PT = [Buf(), Buf()]
        B_pooled = Buf()
        B_stage = [Buf(), Buf()]
        B_o1 = Buf()
        B_osb = Buf()
        B_ob = Buf()
        B_fin = Buf()

        try:
            setup_toks = []

            def setup_dma(eng, out, in_):
                t = R.op(eng, lambda e, o=out, i=in_: e.dma_start(out=o, in_=i),
                         sem=(SSETP if eng == "pool" else SSET), inc=16)
                setup_toks.append(t)
                return t

            setup_dma("pool", idb, ident_in.ap())
            setup_dma("pool", maskb, mask_in.ap())
            setup_dma("pool", bandsb, bands_in.ap())
            setup_dma("pool", pw.rearrange("p (c d) -> p c d", c=2),
                      poolw_in.ap().rearrange("(c p) d -> p c d", p=128))
            setup_dma("sp", psc, psc_in.ap())
            setup_dma("sp", gTm, gT_mix.ap())
            setup_dma("sp", sub_bc, bass.AP(subln_in, 0, [[0, 128], [1, 256]]))
            setup_dma("sp", lq, bass.AP(lam_in, 0, [[0, 128], [1, 512]]))

            def cast_w(src, dst, nchunks, kc):
                for n in range(nchunks):
                    o = dst.ap()[n].rearrange("p (k n) -> p k n", n=512)
                    i = src.ap()[:, n * 512:(n + 1) * 512].rearrange("(k p) n -> p k n", p=128)
                    R.op("pool", lambda e, o=o, i=i: e.dma_start(out=o, in_=i), sem=SWC, inc=16)

            cast_w(w_o_in, s_wo, 4, 16)
            cast_w(wq_in, s_wq, 4, 16)
            cast_w(wkv_in, s_wkv, 8, 16)
            cast_w(wo_in, s_wox, 4, 16)
            cast_w(wg_in, s_wg, 11, 16)
            cast_w(wu_in, s_wu, 11, 16)
            cast_w(wd_in, s_wd, 4, KD)
            T_wcast = (SWC, SWC.n)

            if STOP_AT < 1:
                raise _Stop()
            SETUP_ALL = [(SSET, SSET.n), (SSETP, SSETP.n)]

            t = R.op("dve", lambda e: e.memset(neghalf, -0.5), mark=True)
            t = R.op("dve", lambda e: e.memset(onesb, 1.0), mark=True)
            t_ones = R.op("dve", lambda e: e.memset(Vx3[:, :, 256:257], 1.0), mark=True)
            t_sub = R.op("dve", lambda e: e.tensor_scalar(out=sub_bc, in0=sub_bc, scalar1=1.0 - LAM_INIT,
                                                          scalar2=None, op0=ALU.mult),
                         waits=SETUP_ALL, mark=True)
            R.op("dve", lambda e: e.scalar_tensor_tensor(out=junkd[:, 0:128], in0=lq[:, 0:128], scalar=1.0,
                                                         in1=lq[:, 128:256], op0=ALU.mult, op1=ALU.mult,
                                                         accum_out=dots[:, 0:1]), mark=True)
            t = R.op("dve", lambda e: e.scalar_tensor_tensor(out=junkd[:, 128:256], in0=lq[:, 256:384], scalar=1.0,
                                                             in1=lq[:, 384:512], op0=ALU.mult, op1=ALU.mult,
                                                             accum_out=dots[:, 1:2]), mark=True)
            B_junkd.wrote(t)
            t = R.op("act", lambda e: e.activation(out=ee, in_=dots, func=AF.Exp), waits=[t], mark=True)
            t = R.op("dve", lambda e: e.tensor_tensor(out=tmp1, in0=ee[:, 1:2], in1=ee[:, 0:1], op=ALU.subtract),
                     waits=[t], mark=True)
            t_lam = R.op("dve", lambda e: e.tensor_scalar(out=neglam, in0=tmp1, scalar1=-LAM_INIT, scalar2=None,
                                                          op0=ALU.add), waits=[t], mark=True)

            for c in range(16):
                sl = c % 2
                tl = R.op("sp", lambda e, c=c, sl=sl: e.dma_start(out=xin[sl][:, 0:1024],
                                                                   in_=w_in.ap()[c * 128:(c + 1) * 128, :]),
                          waits=B_xin[sl].wdeps(), sem=SX[sl], inc=16)
                B_xin[sl].wrote(tl)
                tw = R.op("dve", lambda e, c=c, sl=sl: e.tensor_scalar(out=Wb3[:, c, :], in0=xin[sl][:, 0:1024],
                                                                       scalar1=gTm[:, c:c + 1], scalar2=None,
                                                                       op0=ALU.mult),
                          waits=[tl] + SETUP_ALL, mark=True)
                B_xin[sl].read(tw)
            T_wb = tw

            if STOP_AT < 2:
                raise _Stop()
            SC_ATT = 128.0 ** -0.5
            stage_done = []
            cc_toks = []
            deferred = []
            st_ctr = [0]
            pt_ctr = [0]

            def proj_tile(i):
                sl = i % 2
                tl = R.op("sp", lambda e: e.dma_start(out=xin[sl], in_=x_full.ap()[i * 128:(i + 1) * 128, :]),
                          waits=B_xin[sl].wdeps(), sem=SX[sl], inc=16)
                B_xin[sl].wrote(tl)
                t1 = R.op("act", lambda e: e.activation(out=junk, in_=xin[sl], func=AF.Square,
                                                        accum_out=ssq[:, i:i + 1]), waits=[tl] + B_junk.wdeps(), mark=True)
                B_junk.wrote(t1)
                t2 = R.op("dve", lambda e: e.tensor_scalar(out=rstd[:, i:i + 1], in0=ssq[:, i:i + 1], scalar1=1.0 / D,
                                                           scalar2=EPS, op0=ALU.mult, op1=ALU.add),
                          waits=[t1], mark=True)
                t3 = R.op("act", lambda e: e.activation(out=rstd[:, i:i + 1], in_=rstd[:, i:i + 1], func=AF.Ln),
                          waits=[t2], mark=True)
                t_rstd = R.op("act", lambda e: e.activation(out=rstd[:, i:i + 1], in_=rstd[:, i:i + 1], func=AF.Exp,
                                                            scale=-0.5), waits=[t3], mark=True)
                t4 = R.op("dve", lambda e: e.tensor_copy(out=hs[:, 0:1024], in_=xin[sl][:, 0:1024]),
                          waits=[tl] + B_hs.wdeps(), mark=True)
                t5 = R.op("dve", lambda e: e.tensor_copy(out=hs[:, 1024:2048], in_=xin[sl][:, 1024:2048]),
                          mark=True)
                B_hs.wrote(t4, t5)
                B_xin[sl].read(t1, t4, t5)
                hps = bank16(4, 2)
                w = B_hs.rdeps() + BK[4].wdeps() + BK[5].wdeps() + SETUP_ALL
                for c in range(16):
                    tp = R.op("pe", lambda e, c=c: e.transpose(out=hps[:, c * 128:(c + 1) * 128],
                                                                in_=hs[:, c * 128:(c + 1) * 128], identity=idb),
                              waits=w if c == 0 else (), mark=(c == 15))
                B_hs.read(tp)
                BK[4].wrote(tp)
                BK[5].wrote(tp)
                ta = R.op("act", lambda e: e.activation(out=hT[:, 0:1024], in_=hps[:, 0:1024], func=AF.Copy),
                          waits=[tp] + B_hT.wdeps(), mark=True)
                tb = R.op("dve", lambda e: e.tensor_copy(out=hT[:, 1024:2048], in_=hps[:, 1024:2048]),
                          waits=[tp] + B_hT.wdeps(), mark=True)
                BK[4].read(ta)
                BK[5].read(tb)
                B_hT.wrote(ta, tb)
                w = B_hT.rdeps() + BK[6].wdeps() + BK[7].wdeps() + [T_wb]
                first = True
                for c in range(16):
                    for n in range(2):
                        tp = R.op("pe", lambda e, c=c, n=n: e.matmul(bank(6 + n), lhsT=hT[:, c * 128:(c + 1) * 128],
                                                                     rhs=Wb3[:, c, n * 512:(n + 1) * 512],
                                                                     start=(c == 0), stop=(c == 15)),
                                  waits=w if first else (), mark=(c == 15 and n == 1))
                        first = False
                B_hT.read(tp)
                BK[6].wrote(tp)
                BK[7].wrote(tp)
                us = i % 3
                t6 = R.op("dve", lambda e: e.tensor_scalar(out=qk_tm, in0=bank(6), scalar1=rstd[:, i:i + 1],
                                                           scalar2=None, op0=ALU.mult),
                          waits=[tp, t_rstd] + B_qk.wdeps(), mark=True)
                B_qk.wrote(t6)
                BK[6].read(t6)
                t7 = R.op("act", lambda e: e.activation(out=Vx3[:, i, 0:256], in_=bank(7)[:, 0:256], func=AF.Identity,
                                                        scale=rstd[:, i:i + 1]),
                          waits=[tp, t_rstd, t_ones], mark=True)
                t8 = R.op("act", lambda e: e.activation(out=ut[us], in_=bank(7)[:, 256:512], func=AF.Identity,
                                                        scale=rstd[:, i:i + 1]),
                          waits=[tp, t_rstd] + B_ut[us].wdeps(), mark=True)
                B_ut[us].wrote(t8)
                BK[7].read(t7, t8)
                qps = bank16(4)
                w = B_qk.rdeps() + BK[4].wdeps()
                for j in range(4):
                    tp = R.op("pe", lambda e, j=j: e.transpose(out=qps[:, j * 128:(j + 1) * 128],
                                                                in_=qk_tm[:, j * 128:(j + 1) * 128], identity=idb),
                              waits=w if j == 0 else (), mark=(j == 3))
                B_qk.read(tp)
                BK[4].wrote(tp)
                qs = i % 2
                t9 = R.op("dve", lambda e: e.tensor_copy(out=qT[qs], in_=qps[:, 0:256]),
                          waits=[tp] + B_qT[qs].wdeps(), mark=True)
                B_qT[qs].wrote(t9)
                t10 = R.op("dve", lambda e: e.tensor_copy(out=kT3[:, :, i * 128:(i + 1) * 128],
                                                          in_=qps[:, 256:512].rearrange("p (h t) -> p h t", h=2)),
                           waits=[tp], mark=True)
                BK[4].read(t9, t10)
                pps = bank(5)
                w = B_ut[us].rdeps() + BK[5].wdeps() + SETUP_ALL
                if i > 0:
                    w += B_ut[(i - 1) % 3].rdeps()
                first = True
                for cc in range(2):
                    b0 = bandsb[:, 0:128] if i == 0 else bandsb[:, 128:256]
                    tp = R.op("pe", lambda e, cc=cc, b0=b0: e.matmul(pps[:, cc * 128:(cc + 1) * 128],
                                                                     lhsT=ut[us][:, cc * 128:(cc + 1) * 128], rhs=b0,
                                                                     start=True, stop=(i == 0)),
                              waits=w if first else (), mark=(i == 0 and cc == 1))
                    first = False
                    if i > 0:
                        up = ut[(i - 1) % 3]
                        tp = R.op("pe", lambda e, cc=cc, up=up: e.matmul(pps[:, cc * 128:(cc + 1) * 128],
                                                                         lhsT=up[:, cc * 128:(cc + 1) * 128],
                                                                         rhs=bandsb[:, 256:384], start=False, stop=True),
                                  mark=(cc == 1))
                B_ut[us].read(tp)
                if i > 0:
                    B_ut[(i - 1) % 3].read(tp)
                BK[5].wrote(tp)
                t11 = R.op("dve", lambda e: e.tensor_copy(out=pooledT, in_=pps[:, 0:256]),
                           waits=[tp] + B_pooled.wdeps(), mark=True)
                B_pooled.wrote(t11)
                BK[5].read(t11)
                w = [t11] + BK[5].wdeps()
                first = True
                for dc in range(2):
                    for cc in range(2):
                        tp = R.op("pe", lambda e, dc=dc, cc=cc: e.matmul(
                            pps[:, 256 + dc * 128:256 + (dc + 1) * 128],
                            lhsT=pw[:, cc * 256 + dc * 128:cc * 256 + (dc + 1) * 128],
                            rhs=pooledT[:, cc * 128:(cc + 1) * 128], start=(cc == 0), stop=(cc == 1)),
                            waits=w if first else (), mark=(dc == 1 and cc == 1))
                        first = False
                B_pooled.read(tp)
                BK[5].wrote(tp)
                sb = (i // TS) % 2
                tc = i % TS
                st3 = stage[sb].rearrange("p (a t) -> p a t", a=4)
                wst = B_stage[sb].wdeps() if tc == 0 else []
                for dc in range(2):
                    t12 = R.op("dve", lambda e, dc=dc: e.tensor_scalar(
                        out=st3[:, 2 + dc, tc * 128:(tc + 1) * 128], in0=pps[:, 256 + dc * 128:256 + (dc + 1) * 128],
                        scalar1=psc[:, dc:dc + 1], scalar2=None, op0=ALU.mult),
                        waits=[tp] + wst, mark=True)
                BK[5].read(t12)
                if tc == 0:
                    B_stage[sb].wrote(t12)
                else:
                    B_stage[sb].w.append(t12)
                return (t7, t10)

            def attn_tile(i, t_kv):
                qs = i % 2
                groups = [(k0, min(2, i + 1 - k0)) for k0 in range(0, i + 1, 2)]
                pend = None

                def emit_qk(k0, n):
                    sbk = 2 + (st_ctr[0] % 2)
                    st_ctr[0] += 1
                    stp = bank(sbk)
                    w = B_qT[qs].rdeps() + BK[sbk].wdeps() + SETUP_ALL + list(t_kv)
                    first = True
                    for a in range(n):
                        kb = k0 + a
                        for h in range(2):
                            diag = (kb == i)
                            col = (a * 2 + h) * 128
                            tp = R.op("pe", lambda e, kb=kb, h=h, col=col, diag=diag: e.matmul(
                                stp[:, col:col + 128], lhsT=kT3[:, h, kb * 128:(kb + 1) * 128],
                                rhs=qT[qs][:, h * 128:(h + 1) * 128], start=True, stop=not diag),
                                waits=w if first else (), mark=(not diag and a == n - 1 and h == 1))
                            first = False
                            if diag:
                                tp = R.op("pe", lambda e, col=col: e.matmul(stp[:, col:col + 128], lhsT=idb, rhs=maskb,
                                                                            start=False, stop=True),
                                          mark=(h == 1))
                    BK[sbk].wrote(tp)
                    B_qT[qs].read(tp)
                    return (sbk, tp, k0, n)

                def emit_exp_pv(sbk, tqk, k0, n):
                    pb = pt_ctr[0] % 2
                    pt_ctr[0] += 1
                    te = R.op("act", lambda e: e.activation(out=PT[pb][:, 0:n * 256], in_=bank(sbk)[:, 0:n * 256],
                                                            func=AF.Exp, scale=SC_ATT),
                              waits=[tqk] + B_PT[pb].wdeps(), mark=True)
                    B_PT[pb].wrote(te)
                    BK[sbk].read(te)
                    w = [te, t_ones] + (BK[0].wdeps() + BK[1].wdeps() if k0 == 0 else [])
                    first = True
                    for a in range(n):
                        kb = k0 + a
                        for h in range(2):
                            col = (a * 2 + h) * 128
                            tp = R.op("pe", lambda e, kb=kb, h=h, col=col: e.matmul(
                                bank(h)[:, 0:257], lhsT=PT[pb][:, col:col + 128], rhs=Vx3[:, kb, 0:257],
                                start=(kb == 0), stop=(kb == i)),
                                waits=w if first else (), mark=(a == n - 1 and h == 1))
                            first = False
                    B_PT[pb].read(tp)
                    return tp

                tlast = None
                for (k0, n) in groups:
                    cur = emit_qk(k0, n)
                    if pend is not None:
                        tlast = emit_exp_pv(*pend)
                    pend = cur
                tlast = emit_exp_pv(*pend)
                BK[0].wrote(tlast)
                BK[1].wrote(tlast)
                w = [tlast, t_lam, t_sub] + B_fin.wdeps() + B_o1.wdeps() + B_osb.wdeps() + B_ob.wdeps()
                ta = R.op("dve", lambda e: e.reciprocal(out=fin[:, 0:1], in_=bank(0)[:, 256:257]), waits=w, mark=True)
                tb = R.op("dve", lambda e: e.reciprocal(out=fin[:, 1:2], in_=bank(1)[:, 256:257]), mark=True)
                tcx = R.op("dve", lambda e: e.tensor_tensor(out=fin[:, 2:3], in0=fin[:, 1:2], in1=neglam, op=ALU.mult),
                           waits=[tb], mark=True)
                td = R.op("dve", lambda e: e.tensor_scalar(out=o1, in0=bank(0)[:, 0:256], scalar1=fin[:, 0:1],
                                                           scalar2=None, op0=ALU.mult), waits=[ta], mark=True)
                te = R.op("dve", lambda e: e.scalar_tensor_tensor(out=osb, in0=bank(1)[:, 0:256], scalar=fin[:, 2:3],
                                                                  in1=o1, op0=ALU.mult, op1=ALU.add),
                          waits=[tcx, td], mark=True)
                BK[0].read(te)
                BK[1].read(te)
                tf = R.op("dve", lambda e: e.scalar_tensor_tensor(out=junkd[:, 0:256], in0=osb, scalar=1.0, in1=osb,
                                                                  op0=ALU.mult, op1=ALU.mult, accum_out=fin[:, 3:4]),
                          waits=[te] + B_junkd.wdeps(), mark=True)
                B_junkd.wrote(tf)
                tg = R.op("dve", lambda e: e.tensor_scalar(out=fin[:, 4:5], in0=fin[:, 3:4], scalar1=1.0 / 256,
                                                           scalar2=EPS, op0=ALU.mult, op1=ALU.add),
                          waits=[tf], mark=True)
                th = R.op("act", lambda e: e.activation(out=fin[:, 5:6], in_=fin[:, 4:5], func=AF.Ln),
                          waits=[tg], mark=True)
                ti = R.op("act", lambda e: e.activation(out=fin[:, 6:7], in_=fin[:, 5:6], func=AF.Exp, scale=-0.5),
                          waits=[th], mark=True)
                tj = R.op("dve", lambda e: e.scalar_tensor_tensor(out=ob, in0=osb, scalar=fin[:, 6:7], in1=sub_bc,
                                                                  op0=ALU.mult, op1=ALU.mult),
                          waits=[ti], mark=True)
                B_fin.wrote(tj)
                B_o1.wrote(tj)
                B_osb.wrote(tj)
                B_ob.wrote(tj)
                deferred.append(i)

            def finalize_pe(i):
                ops = bank16(6)
                w = B_ob.rdeps() + BK[6].wdeps()
                for dc in range(2):
                    tp = R.op("pe", lambda e, dc=dc: e.transpose(out=ops[:, dc * 128:(dc + 1) * 128],
                                                                  in_=ob[:, dc * 128:(dc + 1) * 128], identity=idb),
                              waits=w if dc == 0 else (), mark=(dc == 1))
                B_ob.read(tp)
                BK[6].wrote(tp)
                sb = (i // TS) % 2
                tc = i % TS
                st3 = stage[sb].rearrange("p (a t) -> p a t", a=4)
                tk = R.op("dve", lambda e: e.tensor_copy(out=st3[:, 0:2, tc * 128:(tc + 1) * 128],
                                                         in_=ops[:, 0:256].rearrange("p (a t) -> p a t", a=2)),
                          waits=[tp], mark=True)
                BK[6].read(tk)
                B_stage[sb].w.append(tk)
                if tc == TS - 1:
                    t0 = (i // TS) * TS * 128
                    pj, pc = t0 // PIECE, t0 % PIECE
                    tdma = R.op("pool", lambda e: e.dma_start(
                        out=xch_in[pj].ap().rearrange("(a p) t -> p a t", p=128)[:, :, pc:pc + TS * 128], in_=st3),
                        waits=B_stage[sb].rdeps(), sem=SST[sb], inc=16)
                    B_stage[sb].read(tdma)
                    stage_done.append(tdma)
                    if (t0 + TS * 128) % PIECE == 0:
                        for t_ in stage_done[-2:]:
                            R.wait("pool", t_)
                        cc_toks.append(R.op("pool", lambda e: e.collective_compute(
                            "AllGather", ALU.bypass, replica_groups=[[0, 1, 2, 3], [4, 5, 6, 7]],
                            ins=[xch_in[pj].ap()], outs=[xch_out.ap()[pj]]), sem=SCC, inc=1))

            for i in range(NT):
                t_kv = proj_tile(i)
                while deferred:
                    finalize_pe(deferred.pop(0))
                attn_tile(i, t_kv)
            while deferred:
                finalize_pe(deferred.pop(0))

            if STOP_AT < 3:
                raise _Stop()
            assert len(cc_toks) == NP
            T_cc = cc_toks[-1]
            for eng in ("pe", "act", "dve", "pool", "sp"):
                R.wait(eng, T_cc)

            if STOP_AT < 4:
                raise _Stop()
            A.off = common_end
            xres = A.alloc(TG * 2048, F32)
            xres3 = xres.rearrange("p (t f) -> p t f", f=2048)
            gb_x = A.alloc(2048, F32)
            gb_f = A.alloc(2048, F32)
            gb_o = A.alloc(2048, F32)
            bufA = A.alloc(16 * G)
            bufB = A.alloc(16 * G)
            NFH = 24
            bufC = A.alloc(NFH * G)
            kTm = A.alloc(16 * MEM)
            vm = A.alloc(2 * 2048)
            ring = [A.alloc(8192) for _ in range(NW)]
            hs2 = A.alloc(2048)
            PT2 = [A.alloc(G) for _ in range(2)]
            rl = A.alloc(G, F32)
            sg = [A.alloc(G) for _ in range(2)]
            st2 = A.alloc(64, F32)
            bufA3 = bufA.rearrange("p (c t) -> p c t", t=G)
            bufB3 = bufB.rearrange("p (c t) -> p c t", t=G)
            bufC3 = bufC.rearrange("p (c t) -> p c t", t=G)
            kTm3 = kTm.rearrange("p (c m) -> p c m", m=MEM)
            vm3 = vm.rearrange("p (c f) -> p c f", f=2048)

            B_xres = [Buf() for _ in range(TG)]
            B_A = Buf()
            B_B = Buf()
            B_C = Buf()
            B_ring = [Buf() for _ in range(NW)]
            B_hs2 = Buf()
            B_PT2 = [Buf(), Buf()]
            B_rl = Buf()
            B_sg = [Buf(), Buf()]
            for b in BK:
                b.w = []
                b.r = []
            bk_ctr = [0]
            ring_ctr = [0]
            st2_ctr = [0]

            def next_bank():
                b = bk_ctr[0] % 8
                bk_ctr[0] += 1
                return b

            def load_w(src_ap):
                k = ring_ctr[0] % NW
                ring_ctr[0] += 1
                L = src_ap.shape[1]
                t = R.op("sp", lambda e: e.dma_start(out=ring[k][:, 0:L], in_=src_ap),
                         waits=B_ring[k].wdeps() + [T_wcast], sem=SR[k], inc=16)
                B_ring[k].wrote(t)
                return k, t

            ld = R.op("sp", lambda e: e.dma_start(out=gb_x, in_=bass.AP(g_xat, 0, [[0, 128], [1, 2048]])),
                      sem=SMEM, inc=16)
            ld = R.op("sp", lambda e: e.dma_start(out=gb_f, in_=bass.AP(g_ffn, 0, [[0, 128], [1, 2048]])),
                      sem=SMEM, inc=16)
            ld = R.op("sp", lambda e: e.dma_start(out=gb_o, in_=bass.AP(g_mem, 0, [[0, 128], [1, 2048]])),
                      sem=SMEM, inc=16)
            T_gb = ld

            def norm_transpose(src_tile_ap, gb, dst3, tcol, src_buf, dst_buf, ncols=128, extra_w=()):
                k = st2_ctr[0] % 16
                st2_ctr[0] += 1
                c0 = k * 4
                t1 = R.op("act", lambda e: e.activation(out=junk, in_=src_tile_ap, func=AF.Square,
                                                        accum_out=st2[:, c0:c0 + 1]),
                          waits=src_buf.rdeps() + list(extra_w) + B_junk.wdeps(), mark=True)
                B_junk.wrote(t1)
                t2 = R.op("dve", lambda e: e.tensor_scalar(out=st2[:, c0 + 1:c0 + 2], in0=st2[:, c0:c0 + 1],
                                                           scalar1=1.0 / D, scalar2=EPS, op0=ALU.mult, op1=ALU.add),
                          waits=[t1], mark=True)
                t3a = R.op("act", lambda e: e.activation(out=st2[:, c0 + 3:c0 + 4], in_=st2[:, c0 + 1:c0 + 2],
                                                         func=AF.Ln), waits=[t2], mark=True)
                t3 = R.op("act", lambda e: e.activation(out=st2[:, c0 + 2:c0 + 3], in_=st2[:, c0 + 3:c0 + 4],
                                                        func=AF.Exp, scale=-0.5), waits=[t3a], mark=True)
                t4 = R.op("dve", lambda e: e.scalar_tensor_tensor(out=hs2, in0=src_tile_ap, scalar=st2[:, c0 + 2:c0 + 3],
                                                                  in1=gb, op0=ALU.mult, op1=ALU.mult),
                          waits=[t3, T_gb] + src_buf.rdeps() + B_hs2.wdeps(), mark=True)
                B_hs2.wrote(t4)
                src_buf.read(t1, t4)
                toks = []
                for half in range(2):
                    b = next_bank()
                    hp = bank16(b)
                    w = [t4] + BK[b].wdeps()
                    for c8 in range(8):
                        c = half * 8 + c8
                        tp = R.op("pe", lambda e, c=c, c8=c8, hp=hp: e.transpose(
                            out=hp[:, c8 * 128:(c8 + 1) * 128], in_=hs2[:, c * 128:(c + 1) * 128], identity=idb),
                            waits=w if c8 == 0 else (), mark=(c8 == 7))
                    BK[b].wrote(tp)
                    eng = "act" if half == 0 else "dve"
                    dst = dst3[:, half * 8:(half + 1) * 8, tcol * 128:(tcol + 1) * 128]
                    src = hp[:, 0:1024].rearrange("p (c t) -> p c t", c=8)
                    if eng == "act":
                        te = R.op("act", lambda e, dst=dst, src=src: e.activation(out=dst, in_=src, func=AF.Copy),
                                  waits=[tp] + dst_buf.wdeps(), mark=True)
                    else:
                        te = R.op("dve", lambda e, dst=dst, src=src: e.tensor_copy(out=dst, in_=src),
                                  waits=[tp] + dst_buf.wdeps(), mark=True)
                    BK[b].read(te)
                    toks.append(te)
                B_hs2.read(tp)
                return toks

            B_mem = Buf()
            mT3 = bufA.rearrange("p (c t) -> p c t", t=G)
            mtoks = []
            for mt in range(2):
                sl = mt
                tl = R.op("sp", lambda e, mt=mt: e.dma_start(out=xres3[:, mt, :], in_=mem_in.ap()[mt * 128:(mt + 1) * 128, :]),
                          sem=SY[mt], inc=16)
                B_xres[mt].wrote(tl)
                mtoks += norm_transpose(xres3[:, mt, :], gb_o, mT3, mt, B_xres[mt], B_A)
            B_A.wrote(*mtoks)
            for n in range(4):
                k, tw = load_w(s_wkv.ap()[n])
                wr = ring[k].rearrange("p (c n) -> p c n", n=512)
                for j in range(4):
                    b = next_bank()
                    w = [tw] + B_A.rdeps() + BK[b].wdeps()
                    for c in range(16):
                        tp = R.op("pe", lambda e, c=c, j=j, b=b, wr=wr: e.matmul(
                            bank(b)[:, 0:MEM], lhsT=wr[:, c, j * 128:(j + 1) * 128], rhs=mT3[:, c, 0:MEM],
                            start=(c == 0), stop=(c == 15)), waits=w if c == 0 else (), mark=(c == 15))
                    BK[b].wrote(tp)
                    te = R.op("act", lambda e, n=n, j=j, b=b: e.activation(out=kTm3[:, n * 4 + j, :],
                                                                            in_=bank(b)[:, 0:MEM], func=AF.Copy),
                              waits=[tp], mark=True)
                    BK[b].read(te)
                B_ring[k].read(tp)
            T_kTm = te
            for n in range(4):
                k, tw = load_w(s_wkv.ap()[4 + n])
                wr = ring[k].rearrange("p (c n) -> p c n", n=512)
                for mc in range(2):
                    b = next_bank()
                    w = [tw] + B_A.rdeps() + BK[b].wdeps()
                    for c in range(16):
                        tp = R.op("pe", lambda e, c=c, mc=mc, b=b, wr=wr: e.matmul(
                            bank(b), lhsT=mT3[:, c, mc * 128:(mc + 1) * 128], rhs=wr[:, c, :],
                            start=(c == 0), stop=(c == 15)), waits=w if c == 0 else (), mark=(c == 15))
                    BK[b].wrote(tp)
                    te = R.op("dve", lambda e, n=n, mc=mc, b=b: e.tensor_copy(out=vm3[:, mc, n * 512:(n + 1) * 512],
                                                                               in_=bank(b)), waits=[tp], mark=True)
                    BK[b].read(te)
                B_ring[k].read(tp)
            T_vm = te
            B_A.read(tp)
            T_gbfin = R.op("sp", lambda e: e.dma_start(out=gb_o, in_=bass.AP(g_fin, 0, [[0, 128], [1, 2048]])),
                           waits=[T_vm], sem=SMEM, inc=16)
            final_stores = []

            if STOP_AT < 5:
                raise _Stop()
            core = [None]
            SC_X = 512.0 ** -0.5

            def proj_tokmajor(src3, src_buf, wsrc, resid_add):
                for n in range(4):
                    k, tw = load_w(wsrc.ap()[n])
                    wr = ring[k].rearrange("p (c n) -> p c n", n=512)
                    for t in range(TG):
                        b = next_bank()
                        w = [tw] + src_buf.rdeps() + BK[b].wdeps()
                        for c in range(16):
                            tp = R.op("pe", lambda e, c=c, t=t, b=b, wr=wr: e.matmul(
                                bank(b), lhsT=src3[:, c, t * 128:(t + 1) * 128], rhs=wr[:, c, :],
                                start=(c == 0), stop=(c == 15)), waits=w if c == 0 else (), mark=(c == 15))
                        BK[b].wrote(tp)
                        te = R.op("dve", lambda e, t=t, n=n, b=b: e.tensor_tensor(
                            out=xres3[:, t, n * 512:(n + 1) * 512], in0=bank(b),
                            in1=xres3[:, t, n * 512:(n + 1) * 512], op=ALU.add),
                            waits=[tp] + B_xres[t].wdeps(), mark=True)
                        BK[b].read(te)
                        B_xres[t].wrote(te)
                    B_ring[k].read(tp)
                src_buf.read(tp)

            for gi in range(NG):
                tok0 = gi * G
                for t in range(TG):
                    tl = R.op("sp", lambda e, t=t, tok0=tok0: e.dma_start(out=xres3[:, t, :],
                                                               in_=x_sl.ap()[tok0 + t * 128:tok0 + (t + 1) * 128, :]),
                              waits=B_xres[t].wdeps(), sem=SXR, inc=16)
                for t in range(TG):
                    B_xres[t].wrote(tl)
                def ld_ot(e, tok0=tok0):
                    if core[0] is None:
                        core[0] = e.partition_id()
                    rank = core[0] % 4
                    pidx = (rank * (SL // PIECE) + tok0 // PIECE) * 16
                    pcol = tok0 % PIECE
                    src = xch_out.ap().rearrange("j (c p) t -> p (j c) t", p=128)[:, bass.ds(pidx, 16), pcol:pcol + G]
                    return e.dma_start(out=bufB3, in_=src)
                tl = R.op("pool", ld_ot, waits=B_B.wdeps() + [T_cc], sem=SOT, inc=16)
                B_B.wrote(tl)
                proj_tokmajor(bufB3, B_B, s_wo, True)
                toks = []
                for t in range(TG):
                    toks += norm_transpose(xres3[:, t, :], gb_x, bufA3, t, B_xres[t], B_A)
                B_A.wrote(*toks)
                first_d = True
                for n in range(4):
                    k, tw = load_w(s_wq.ap()[n])
                    wr = ring[k].rearrange("p (c n) -> p c n", n=512)
                    for j in range(4):
                        b = next_bank()
                        w = [tw] + B_A.rdeps() + BK[b].wdeps()
                        for c in range(16):
                            tp = R.op("pe", lambda e, c=c, j=j, b=b, wr=wr: e.matmul(
                                bank(b)[:, 0:G], lhsT=wr[:, c, j * 128:(j + 1) * 128], rhs=bufA3[:, c, :],
                                start=(c == 0), stop=(c == 15)), waits=w if c == 0 else (), mark=(c == 15))
                        BK[b].wrote(tp)
                        te = R.op("act", lambda e, n=n, j=j, b=b: e.activation(out=bufB3[:, n * 4 + j, :],
                                                                                in_=bank(b)[:, 0:G], func=AF.Copy),
                                  waits=[tp] + (B_B.wdeps() if first_d else []), mark=True)
                        first_d = False
                        BK[b].read(te)
                    B_ring[k].read(tp)
                B_A.read(tp)
                B_B.wrote(te)
                first_e = True
                for hh in range(4):
                    pts = []
                    for mc in range(2):
                        b = next_bank()
                        w = B_B.rdeps() + BK[b].wdeps() + [T_kTm]
                        for dd in range(4):
                            tp = R.op("pe", lambda e, dd=dd, mc=mc, b=b, hh=hh: e.matmul(
                                bank(b)[:, 0:G], lhsT=kTm3[:, hh * 4 + dd, mc * 128:(mc + 1) * 128],
                                rhs=bufB3[:, hh * 4 + dd, :], start=(dd == 0), stop=(dd == 3)),
                                waits=w if dd == 0 else (), mark=(dd == 3))
                        BK[b].wrote(tp)
                        te = R.op("act", lambda e, mc=mc, b=b: e.activation(out=PT2[mc], in_=bank(b)[:, 0:G],
                                                                             func=AF.Exp, scale=SC_X),
                                  waits=[tp] + B_PT2[mc].wdeps(), mark=True)
                        BK[b].read(te)
                        B_PT2[mc].wrote(te)
                        pts.append(te)
                    b = next_bank()
                    w = pts + BK[b].wdeps()
                    for mc in range(2):
                        tp = R.op("pe", lambda e, mc=mc, b=b: e.matmul(bank(b)[:, 0:G], lhsT=onesb, rhs=PT2[mc],
                                                                       start=(mc == 0), stop=(mc == 1)),
                                  waits=w if mc == 0 else (), mark=(mc == 1))
                    BK[b].wrote(tp)
                    trl = R.op("dve", lambda e, b=b: e.reciprocal(out=rl, in_=bank(b)[:, 0:G]),
                               waits=[tp] + B_rl.wdeps(), mark=True)
                    BK[b].read(trl)
                    B_rl.wrote(trl)
                    for dv in range(4):
                        b = next_bank()
                        w = pts + BK[b].wdeps() + [T_vm]
                        for mc in range(2):
                            tp = R.op("pe", lambda e, mc=mc, b=b, dv=dv, hh=hh: e.matmul(
                                bank(b)[:, 0:G], lhsT=vm3[:, mc, (hh * 4 + dv) * 128:(hh * 4 + dv + 1) * 128],
                                rhs=PT2[mc], start=(mc == 0), stop=(mc == 1)),
                                waits=w if mc == 0 else (), mark=(mc == 1))
                        BK[b].wrote(tp)
                        te = R.op("dve", lambda e, b=b, dv=dv, hh=hh: e.tensor_tensor(
                            out=bufA3[:, hh * 4 + dv, :], in0=bank(b)[:, 0:G], in1=rl, op=ALU.mult),
                            waits=[tp, trl] + (B_A.wdeps() if first_e else []), mark=True)
                        first_e = False
                        BK[b].read(te)
                    B_PT2[0].read(tp)
                    B_PT2[1].read(tp)
                    B_rl.read(te)
                B_B.read(tp)
                B_A.wrote(te)
                proj_tokmajor(bufA3, B_A, s_wox, True)
                toks = []
                for t in range(TG):
                    toks += norm_transpose(xres3[:, t, :], gb_f, bufA3, t, B_xres[t], B_A)
                B_A.wrote(*toks)
                f0 = 0
                for nf in (NFH, KD - NFH):
                    first_h = True
                    for wc in range(nf // 4):
                        n = (f0 // 4) + wc
                        kg, twg = load_w(s_wg.ap()[n])
                        ku, twu = load_w(s_wu.ap()[n])
                        wg = ring[kg].rearrange("p (c n) -> p c n", n=512)
                        wu = ring[ku].rearrange("p (c n) -> p c n", n=512)
                        for j in range(4):
                            fl = wc * 4 + j
                            bg = next_bank()
                            w = [twg] + B_A.rdeps() + BK[bg].wdeps()
                            for c in range(16):
                                tpg = R.op("pe", lambda e, c=c, j=j, b=bg, wr=wg: e.matmul(
                                    bank(b)[:, 0:G], lhsT=wr[:, c, j * 128:(j + 1) * 128], rhs=bufA3[:, c, :],
                                    start=(c == 0), stop=(c == 15)), waits=w if c == 0 else (), mark=(c == 15))
                            BK[bg].wrote(tpg)
                            bu = next_bank()
                            w = [twu] + BK[bu].wdeps()
                            for c in range(16):
                                tpu = R.op("pe", lambda e, c=c, j=j, b=bu, wr=wu: e.matmul(
                                    bank(b)[:, 0:G], lhsT=wr[:, c, j * 128:(j + 1) * 128], rhs=bufA3[:, c, :],
                                    start=(c == 0), stop=(c == 15)), waits=w if c == 0 else (), mark=(c == 15))
                            BK[bu].wrote(tpu)
                            si = fl % 2
                            ts_ = R.op("act", lambda e, b=bg, si=si: e.activation(out=sg[si], in_=bank(b)[:, 0:G],
                                                                                   func=AF.Silu),
                                       waits=[tpg] + B_sg[si].wdeps(), mark=True)
                            BK[bg].read(ts_)
                            B_sg[si].wrote(ts_)
                            tm = R.op("dve", lambda e, b=bu, si=si, fl=fl: e.tensor_tensor(
                                out=bufC3[:, fl, :], in0=bank(b)[:, 0:G], in1=sg[si], op=ALU.mult),
                                waits=[tpu, ts_] + (B_C.wdeps() if first_h else []), mark=True)
                            first_h = False
                            BK[bu].read(tm)
                            B_sg[si].read(tm)
                        B_ring[kg].read(tpg)
                        B_ring[ku].read(tpu)
                    B_A.read(tpu)
                    B_C.wrote(tm)
                    subs = []
                    kk = 0
                    while kk < nf:
                        ln = min(12, nf - kk)
                        subs.append((kk, ln))
                        kk += ln
                    for n in range(4):
                        banks = [next_bank() for _ in range(TG)]
                        for si_, (k0, ln) in enumerate(subs):
                            src = s_wd.ap()[n][:, (f0 + k0) * 512:(f0 + k0 + ln) * 512]
                            k, tw = load_w(src)
                            wr = ring[k].rearrange("p (c n) -> p c n", n=512)
                            for t in range(TG):
                                b = banks[t]
                                w = [tw] + B_C.rdeps() + (BK[b].wdeps() if si_ == 0 else [])
                                for c in range(ln):
                                    st_f = (si_ == 0 and c == 0)
                                    sp_f = (si_ == len(subs) - 1 and c == ln - 1)
                                    tp = R.op("pe", lambda e, c=c, t=t, b=b, wr=wr, k0=k0, st_f=st_f, sp_f=sp_f: e.matmul(
                                        bank(b), lhsT=bufC3[:, k0 + c, t * 128:(t + 1) * 128], rhs=wr[:, c, :],
                                        start=st_f, stop=sp_f),
                                        waits=w if c == 0 else (), mark=(c == ln - 1))
                                if si_ == len(subs) - 1:
                                    BK[b].wrote(tp)
                            B_ring[k].read(tp)
                        for t in range(TG):
                            b = banks[t]
                            te = R.op("dve", lambda e, t=t, n=n, b=b: e.tensor_tensor(
                                out=xres3[:, t, n * 512:(n + 1) * 512], in0=bank(b),
                                in1=xres3[:, t, n * 512:(n + 1) * 512], op=ALU.add),
                                waits=BK[b].rdeps() + B_xres[t].wdeps(), mark=True)
                            BK[b].read(te)
                            B_xres[t].wrote(te)
                    B_C.read(tp)
                    f0 += nf
                for t in range(TG):
                    k = st2_ctr[0] % 16
                    st2_ctr[0] += 1
                    c0 = k * 4
                    ys = t % 2
                    src = xres3[:, t, :]
                    t1 = R.op("act", lambda e, src=src, c0=c0: e.activation(out=junk, in_=src, func=AF.Square,
                                                                            accum_out=st2[:, c0:c0 + 1]),
                              waits=B_xres[t].rdeps() + B_junk.wdeps(), mark=True)
                    B_junk.wrote(t1)
                    t2 = R.op("dve", lambda e, c0=c0: e.tensor_scalar(out=st2[:, c0 + 1:c0 + 2], in0=st2[:, c0:c0 + 1],
                                                                      scalar1=1.0 / D, scalar2=EPS, op0=ALU.mult,
                                                                      op1=ALU.add), waits=[t1], mark=True)
                    t3a = R.op("act", lambda e, c0=c0: e.activation(out=st2[:, c0 + 3:c0 + 4],
                                                                    in_=st2[:, c0 + 1:c0 + 2], func=AF.Ln),
                               waits=[t2], mark=True)
                    t3 = R.op("act", lambda e, c0=c0: e.activation(out=st2[:, c0 + 2:c0 + 3],
                                                                   in_=st2[:, c0 + 3:c0 + 4], func=AF.Exp,
                                                                   scale=-0.5), waits=[t3a], mark=True)
                    t4 = R.op("dve", lambda e, src=src, c0=c0: e.scalar_tensor_tensor(
                        out=src, in0=src, scalar=st2[:, c0 + 2:c0 + 3], in1=gb_o, op0=ALU.mult, op1=ALU.mult),
                        waits=[t3, T_gbfin] + B_xres[t].wdeps(), mark=True)
                    B_xres[t].wrote(t4)
                    tst = R.op("sp", lambda e, t=t, src=src, tok0=tok0: e.dma_start(
                        out=y_out.ap()[tok0 + t * 128:tok0 + (t + 1) * 128, :], in_=src),
                        waits=[t4], sem=SY[t], inc=16)
                    B_xres[t].read(tst)
                    final_stores.append(tst)

            for t in final_stores[-TG:]:
                R.wait("sp", t)

        except _Stop:
            pass

        @block.tensor
        def _(e):
            for f in R.q["pe"]:
                f(e)

        @block.scalar
        def _(e):
            for f in R.q["act"]:
                f(e)

        @block.vector
        def _(e):
            for f in R.q["dve"]:
                f(e)

        @block.gpsimd
        def _(e):
            for f in R.q["pool"]:
                f(e)

        @block.sync
        def _(e):
            for f in R.q["sp"]:
                f(e)

    return nc


POOL_WINDOWS = (2, 4, 8, 16)


def _bands(w):
    j = np.arange(128)[:, None]
    t = np.arange(128)[None, :]
    inwin = (j <= t) & (j > t - w)
    cnt_first = np.minimum(t + 1, w).astype(np.float32)
    eye = (j == t).astype(np.float32)
    b0f = inwin / cnt_first - eye
    b0 = inwin / np.float32(w) - eye
    b1 = ((j - 128) > (t - w)).astype(np.float32) / np.float32(w)
    return np.ascontiguousarray(np.concatenate([b0f, b0, b1], axis=1).astype(np.float32))


_CACHE = {}


def kernel(x, mem, norm_mix, w_in, lambda_q1, lambda_k1, lambda_q2, lambda_k2, subln, pool_w, pool_scale,
           w_o, norm_xattn, norm_mem, wq_x, wkv_x, wo_x, norm_ffn, w_gate, w_up, w_down, norm_final):
    f = lambda a: np.ascontiguousarray(np.asarray(a, dtype=np.float32))
    x = f(x)
    mem = f(mem)
    B, S, _ = x.shape
    SL = S // 4
    if S not in _CACHE:
        _CACHE[S] = build_program(S)
    nc = _CACHE[S]
    w_in0 = f(w_in)[0]
    w_o0 = f(w_o)[0]
    perm = np.concatenate([np.concatenate([np.arange(r * 256, (r + 1) * 256),
                                           1024 + np.arange(r * 256, (r + 1) * 256)]) for r in range(4)])
    w_o_p = np.ascontiguousarray(w_o0[perm])
    kk = np.arange(128)[:, None]
    qq = np.arange(128)[None, :]
    mask = np.where(kk > qq, np.float32(NEG), np.float32(0.0)).astype(np.float32)
    ident = np.eye(128, dtype=np.float32)
    lam = np.concatenate([f(lambda_q1)[0], f(lambda_k1)[0], f(lambda_q2)[0], f(lambda_k2)[0]])
    shared = {
        "gT_mix": np.ascontiguousarray(f(norm_mix)[0].reshape(16, 128).T),
        "g_xat": f(norm_xattn)[0], "g_mem": f(norm_mem)[0], "g_ffn": f(norm_ffn)[0], "g_fin": f(norm_final),
        "lam_in": np.ascontiguousarray(lam), "subln_in": f(subln)[0],
        "ident_in": ident, "mask_in": mask,
    }
    wfull = {"w_o_p": w_o_p, "wq_x": f(wq_x)[0], "wkv_x": f(wkv_x)[0], "wo_x": f(wo_x)[0],
             "w_gate": f(w_gate)[0], "w_up": f(w_up)[0], "w_down": f(w_down)[0]}
    in_maps = []
    for c in range(8):
        b, r = divmod(c, 4)
        cols = np.concatenate([k * 1024 + np.arange(r * 256, (r + 1) * 256) for k in range(4)])
        m = dict(shared)
        m["x_sl"] = np.ascontiguousarray(x[b, r * SL:(r + 1) * SL])
        m["mem_b"] = mem[b]
        m["w_in_c"] = np.ascontiguousarray(w_in0[:, cols])
        m["poolw_in"] = np.ascontiguousarray(f(pool_w)[0, r])
        m["psc_in"] = np.ascontiguousarray(f(pool_scale)[0, r * 256:(r + 1) * 256].reshape(2, 128).T)
        m["bands_in"] = _bands(POOL_WINDOWS[r])
        m["x_full"] = x[b]
        m.update(wfull)
        in_maps.append(m)
    res = run_bass_kernel_spmd(nc, in_maps, core_ids=list(range(8)))
    out = np.empty((B, S, D), dtype=np.float32)
    for c in range(8):
        b, r = divmod(c, 4)
        out[b, r * SL:(r + 1) * SL] = np.asarray(res.results[c]["y"], dtype=np.float32)
    return out
```

```python
import math
from contextlib import ExitStack
import numpy as np
import concourse.bass as bass
import concourse.mybir as mybir
from concourse.bass_utils import run_bass_kernel_spmd

F32 = mybir.dt.float32
BF16 = mybir.dt.bfloat16
AF = mybir.ActivationFunctionType
ALU = mybir.AluOpType

D = 2048
DFF = 5632
MEM = 256
EPS = 1e-6
LAM_INIT = 0.8 - 0.6 * math.exp(0.0)
NEG = -30000.0
NW = 4
ARENA_KB = 207


class _Stop(Exception):
    pass


STOP_AT = 99


class Sem:
    def __init__(self, h):
        self.h = h
        self.n = 0


class Buf:
    def __init__(self):
        self.w = []
        self.r = []

    def wdeps(self):
        return self.w + self.r

    def rdeps(self):
        return list(self.w)

    def wrote(self, *toks):
        self.w = [t for t in toks if t is not None]
        self.r = []

    def read(self, *toks):
        self.r += [t for t in toks if t is not None]


class Rec:
    def __init__(self):
        self.q = {e: [] for e in ("pe", "act", "dve", "pool", "sp")}
        self.waited = {}
        self.P = {}

    def wait(self, eng, tok):
        if tok is None:
            return
        sem, val = tok
        key = (eng, id(sem))
        if self.waited.get(key, 0) >= val:
            return
        self.waited[key] = val
        self.q[eng].append(lambda e, h=sem.h, v=val: e.wait_ge(h, v))

    def op(self, eng, fn, waits=(), sem=None, inc=1, mark=False):
        for w in waits:
            self.wait(eng, w)
        if mark and sem is None:
            sem = self.P[eng]
        if sem is None:
            self.q[eng].append(lambda e, fn=fn: fn(e))
            return None
        sem.n += inc
        val = sem.n
        self.q[eng].append(lambda e, fn=fn, h=sem.h, k=inc: fn(e).then_inc(h, k))
        return (sem, val)


class Arena:
    def __init__(self, ap):
        self.ap = ap
        self.off = 0
        self.cap = ap.shape[1] * 2

    def alloc(self, nelem, dt=BF16):
        nb = nelem * (4 if dt is F32 else 2)
        nb = (nb + 63) // 64 * 64
        assert self.off + nb <= self.cap, ("arena overflow", self.off, nb, self.cap)
        v = self.ap[:, self.off // 2:(self.off + nb) // 2]
        self.off += nb
        if dt is F32:
            v = v.bitcast(F32)
        return v[:, 0:nelem]


def build_program(S):
    NT = S // 128
    SL = S // 4
    G = min(512, SL)
    TG = G // 128
    NG = SL // G
    TS = min(2, NT)
    nc = bass.Bass("TRN2", target_bir_lowering=False)

    def din(name, shape, dt=F32):
        return nc.dram_tensor(name, list(shape), dt, kind="ExternalInput")

    x_sl = din("x_sl", [SL, D])
    mem_in = din("mem_b", [MEM, D])
    w_in = din("w_in_c", [D, 1024])
    gT_mix = din("gT_mix", [128, 16])
    g_xat = din("g_xat", [D])
    g_mem = din("g_mem", [D])
    g_ffn = din("g_ffn", [D])
    g_fin = din("g_fin", [D])
    lam_in = din("lam_in", [4 * 128])
    subln_in = din("subln_in", [256])
    poolw_in = din("poolw_in", [256, 256])
    psc_in = din("psc_in", [128, 2])
    bands_in = din("bands_in", [128, 3 * 128])
    ident_in = din("ident_in", [128, 128])
    mask_in = din("mask_in", [128, 128])
    x_full = din("x_full", [S, D])
    w_o_in = din("w_o_p", [D, D])
    wq_in = din("wq_x", [D, D])
    wkv_in = din("wkv_x", [D, 2 * D])
    wo_in = din("wo_x", [D, D])
    wg_in = din("w_gate", [D, DFF])
    wu_in = din("w_up", [D, DFF])
    wd_in = din("w_down", [DFF, D])
    y_out = nc.dram_tensor("y", [SL, D], F32, kind="ExternalOutput")

    PIECE = min(1024, SL)
    NP = S // PIECE
    xch_in = [nc.dram_tensor(f"xch_in{j}", [512, PIECE], BF16) for j in range(NP)]
    xch_out = nc.dram_tensor("xch_out", [NP, 2048, PIECE], BF16)
    KD = DFF // 128
    s_wo = nc.dram_tensor("s_wo", [4, 128, 16 * 512], BF16)
    s_wq = nc.dram_tensor("s_wq", [4, 128, 16 * 512], BF16)
    s_wox = nc.dram_tensor("s_wox", [4, 128, 16 * 512], BF16)
    s_wkv = nc.dram_tensor("s_wkv", [8, 128, 16 * 512], BF16)
    s_wg = nc.dram_tensor("s_wg", [11, 128, 16 * 512], BF16)
    s_wu = nc.dram_tensor("s_wu", [11, 128, 16 * 512], BF16)
    s_wd = nc.dram_tensor("s_wd", [4, 128, KD * 512], BF16)

    R = Rec()

    with ExitStack() as es:
        arena_t = es.enter_context(nc.sbuf_tensor("arena", [128, ARENA_KB * 512], BF16))
        ps = es.enter_context(nc.psum_tensor("psum", [128, 4096], F32))

        def mksem(name):
            return es.enter_context(nc.semaphore(name))

        h_pe, h_act, h_dve, h_pool = mksem("p_pe"), mksem("p_act"), mksem("p_dve"), mksem("p_pool")
        h_x0, h_x1, h_set, h_wc = mksem("s_x0"), mksem("s_x1"), mksem("s_set"), mksem("s_wc")
        h_setp = mksem("s_setp")
        h_gc, h_gx = mksem("s_gc"), mksem("s_gx")
        h_st0, h_st1, h_cc = mksem("s_st0"), mksem("s_st1"), mksem("s_cc")
        h_r0, h_r1, h_r2, h_r3 = mksem("s_r0"), mksem("s_r1"), mksem("s_r2"), mksem("s_r3")
        h_xr, h_ot, h_y0, h_y1, h_mem = mksem("s_xr"), mksem("s_ot"), mksem("s_y0"), mksem("s_y1"), mksem("s_mem")
        h_y2, h_y3 = mksem("s_y2"), mksem("s_y3")
        block = es.enter_context(nc.Block())
        R.P = {"pe": Sem(h_pe), "act": Sem(h_act), "dve": Sem(h_dve), "pool": Sem(h_pool)}
        SX = [Sem(h_x0), Sem(h_x1)]
        SSET = Sem(h_set)
        SSETP = Sem(h_setp)
        SGC = Sem(h_gc)
        SGX = Sem(h_gx)
        SWC = Sem(h_wc)
        SST = [Sem(h_st0), Sem(h_st1)]
        SCC = Sem(h_cc)
        SR = [Sem(h_r0), Sem(h_r1), Sem(h_r2), Sem(h_r3)]
        SXR = Sem(h_xr)
        SOT = Sem(h_ot)
        SY = [Sem(h_y0), Sem(h_y1), Sem(h_y2), Sem(h_y3)]
        SMEM = Sem(h_mem)

        A = Arena(arena_t[:])

        def bank(b, n=1):
            return ps[:, b * 512:(b + n) * 512]

        def bank16(b, n=1):
            return bank(b, n).bitcast(BF16)

        BK = [Buf() for _ in range(8)]
        B_junk = Buf()
        B_junkd = Buf()

        idb = A.alloc(128)
        onesb = A.alloc(128)
        small = A.alloc(64, F32)
        neghalf = small[:, 0:1]
        neglam = small[:, 1:2]
        dots = small[:, 2:4]
        ee = small[:, 4:6]
        tmp1 = small[:, 6:7]
        junk = A.alloc(2048)
        junkd = A.alloc(256)
        common_end = A.off

        Wb = A.alloc(16 * 1024)
        kT = A.alloc(2 * S)
        Vx = A.alloc(NT * 258)
        xin = [A.alloc(2048, F32), A.alloc(2048, F32)]
        hs = A.alloc(2048)
        hT = A.alloc(2048)
        qk_tm = A.alloc(512)
        qT = [A.alloc(256), A.alloc(256)]
        ut = [A.alloc(256) for _ in range(3)]
        PT = [A.alloc(512) for _ in range(2)]
        maskb = A.alloc(128)
        bandsb = A.alloc(3 * 128)
        pw = A.alloc(2 * 256)
        psc = A.alloc(2, F32)
        gTm = A.alloc(16, F32)
        sub_bc = A.alloc(256, F32)
        lq = A.alloc(512, F32)
        pooledT = A.alloc(256)
        stage = [A.alloc(4 * TS * 128), A.alloc(4 * TS * 128)]
        o1 = lq[:, 0:256]
        osb = lq[:, 256:512]
        ob = A.alloc(256)
        ssq = A.alloc(NT, F32)
        rstd = A.alloc(NT, F32)
        fin = A.alloc(8, F32)
        kT3 = kT.rearrange("p (h s) -> p h s", h=2)
        Vx3 = Vx.rearrange("p (t c) -> p t c", c=258)
        Wb3 = Wb.rearrange("p (c n) -> p c n", n=1024)

        B_xin = [Buf(), Buf()]
        B_hs = Buf()
        B_hT = Buf()
        B_qk = Buf()
        B_qT = [Buf(), Buf()]
        B_ut = [Buf(), Buf(), Buf()]
        B_PT = [Buf(), Buf()]
        B_pooled = Buf()
        B_stage = [Buf(), Buf()]
        B_o1 = Buf()
        B_osb = Buf()
        B_ob = Buf()
        B_fin = Buf()

        try:
            setup_toks = []

            def setup_dma(eng, out, in_):
                t = R.op(eng, lambda e, o=out, i=in_: e.dma_start(out=o, in_=i),
                         sem=(SSETP if eng == "pool" else SSET), inc=16)
                setup_toks.append(t)
                return t

            setup_dma("pool", idb, ident_in.ap())
            setup_dma("pool", maskb, mask_in.ap())
            setup_dma("pool", bandsb, bands_in.ap())
            setup_dma("pool", pw.rearrange("p (c d) -> p c d", c=2),
                      poolw_in.ap().rearrange("(c p) d -> p c d", p=128))
            setup_dma("sp", psc, psc_in.ap())
            setup_dma("sp", gTm, gT_mix.ap())
            setup_dma("sp", sub_bc, bass.AP(subln_in, 0, [[0, 128], [1, 256]]))
            setup_dma("sp", lq, bass.AP(lam_in, 0, [[0, 128], [1, 512]]))

            def cast_w(src, dst, nchunks, kc):
                for n in range(nchunks):
                    o = dst.ap()[n].rearrange("p (k n) -> p k n", n=512)
                    i = src.ap()[:, n * 512:(n + 1) * 512].rearrange("(k p) n -> p k n", p=128)
                    R.op("pool", lambda e, o=o, i=i: e.dma_start(out=o, in_=i), sem=SWC, inc=16)

            cast_w(w_o_in, s_wo, 4, 16)
            cast_w(wq_in, s_wq, 4, 16)
            cast_w(wkv_in, s_wkv, 8, 16)
            cast_w(wo_in, s_wox, 4, 16)
            cast_w(wg_in, s_wg, 11, 16)
            cast_w(wu_in, s_wu, 11, 16)
            cast_w(wd_in, s_wd, 4, KD)
            T_wcast = (SWC, SWC.n)

            if STOP_AT < 1:
                raise _Stop()
            SETUP_ALL = [(SSET, SSET.n), (SSETP, SSETP.n)]

            t = R.op("dve", lambda e: e.memset(neghalf, -0.5), mark=True)
            t = R.op("dve", lambda e: e.memset(onesb, 1.0), mark=True)
            t_ones = R.op("dve", lambda e: e.memset(Vx3[:, :, 256:257], 1.0), mark=True)
            t_sub = R.op("dve", lambda e: e.tensor_scalar(out=sub_bc, in0=sub_bc, scalar1=1.0 - LAM_INIT,
                                                          scalar2=None, op0=ALU.mult),
                         waits=SETUP_ALL, mark=True)
            R.op("dve", lambda e: e.scalar_tensor_tensor(out=junkd[:, 0:128], in0=lq[:, 0:128], scalar=1.0,
                                                         in1=lq[:, 128:256], op0=ALU.mult, op1=ALU.mult,
                                                         accum_out=dots[:, 0:1]), mark=True)
            t = R.op("dve", lambda e: e.scalar_tensor_tensor(out=junkd[:, 128:256], in0=lq[:, 256:384], scalar=1.0,
                                                             in1=lq[:, 384:512], op0=ALU.mult, op1=ALU.mult,
                                                             accum_out=dots[:, 1:2]), mark=True)
            B_junkd.wrote(t)
            t = R.op("act", lambda e: e.activation(out=ee, in_=dots, func=AF.Exp), waits=[t], mark=True)
            t = R.op("dve", lambda e: e.tensor_tensor(out=tmp1, in0=ee[:, 1:2], in1=ee[:, 0:1], op=ALU.subtract),
                     waits=[t], mark=True)
            t_lam = R.op("dve", lambda e: e.tensor_scalar(out=neglam, in0=tmp1, scalar1=-LAM_INIT, scalar2=None,
                                                          op0=ALU.add), waits=[t], mark=True)

            for c in range(16):
                sl = c % 2
                tl = R.op("sp", lambda e, c=c, sl=sl: e.dma_start(out=xin[sl][:, 0:1024],
                                                                   in_=w_in.ap()[c * 128:(c + 1) * 128, :]),
                          waits=B_xin[sl].wdeps(), sem=SX[sl], inc=16)
                B_xin[sl].wrote(tl)
                tw = R.op("dve", lambda e, c=c, sl=sl: e.tensor_scalar(out=Wb3[:, c, :], in0=xin[sl][:, 0:1024],
                                                                       scalar1=gTm[:, c:c + 1], scalar2=None,
                                                                       op0=ALU.mult),
                          waits=[tl] + SETUP_ALL, mark=True)
                B_xin[sl].read(tw)
            T_wb = tw

            if STOP_AT < 2:
                raise _Stop()
            SC_ATT = 128.0 ** -0.5
            stage_done = []
            cc_toks = []
            deferred = []
            st_ctr = [0]
            pt_ctr = [0]

            def proj_tile(i, ret):
                sl = i % 2
                tl = R.op("sp", lambda e: e.dma_start(out=xin[sl], in_=x_full.ap()[i * 128:(i + 1) * 128, :]),
                          waits=B_xin[sl].wdeps(), sem=SX[sl], inc=16)
                B_xin[sl].wrote(tl)
                t1 = R.op("act", lambda e: e.activation(out=junk, in_=xin[sl], func=AF.Square,
                                                        accum_out=ssq[:, i:i + 1]), waits=[tl] + B_junk.wdeps(), mark=True)
                B_junk.wrote(t1)
                t2 = R.op("dve", lambda e: e.tensor_scalar(out=rstd[:, i:i + 1], in0=ssq[:, i:i + 1], scalar1=1.0 / D,
                                                           scalar2=EPS, op0=ALU.mult, op1=ALU.add),
                          waits=[t1], mark=True)
                t3 = R.op("act", lambda e: e.activation(out=rstd[:, i:i + 1], in_=rstd[:, i:i + 1], func=AF.Ln),
                          waits=[t2], mark=True)
                t_rstd = R.op("act", lambda e: e.activation(out=rstd[:, i:i + 1], in_=rstd[:, i:i + 1], func=AF.Exp,
                                                            scale=-0.5), waits=[t3], mark=True)
                t4 = R.op("dve", lambda e: e.tensor_copy(out=hs[:, 0:1024], in_=xin[sl][:, 0:1024]),
                          waits=[tl] + B_hs.wdeps(), mark=True)
                t5 = R.op("dve", lambda e: e.tensor_copy(out=hs[:, 1024:2048], in_=xin[sl][:, 1024:2048]),
                          mark=True)
                B_hs.wrote(t4, t5)
                B_xin[sl].read(t1, t4, t5)
                hps = bank16(4, 2)
                w = B_hs.rdeps() + BK[4].wdeps() + BK[5].wdeps() + SETUP_ALL
                for c in range(16):
                    tp = R.op("pe", lambda e, c=c: e.transpose(out=hps[:, c * 128:(c + 1) * 128],
                                                                in_=hs[:, c * 128:(c + 1) * 128], identity=idb),
                              waits=w if c == 0 else (), mark=(c == 15))
                B_hs.read(tp)
                BK[4].wrote(tp)
                BK[5].wrote(tp)
                ta = R.op("act", lambda e: e.activation(out=hT[:, 0:1024], in_=hps[:, 0:1024], func=AF.Copy),
                          waits=[tp] + B_hT.wdeps(), mark=True)
                tb = R.op("dve", lambda e: e.tensor_copy(out=hT[:, 1024:2048], in_=hps[:, 1024:2048]),
                          waits=[tp] + B_hT.wdeps(), mark=True)
                BK[4].read(ta)
                BK[5].read(tb)
                B_hT.wrote(ta, tb)
                yield
                w = B_hT.rdeps() + BK[6].wdeps() + BK[7].wdeps() + [T_wb]
                first = True
                for c in range(16):
                    for n in range(2):
                        tp = R.op("pe", lambda e, c=c, n=n: e.matmul(bank(6 + n), lhsT=hT[:, c * 128:(c + 1) * 128],
                                                                     rhs=Wb3[:, c, n * 512:(n + 1) * 512],
                                                                     start=(c == 0), stop=(c == 15)),
                                  waits=w if first else (), mark=(c == 15 and n == 1))
                        first = False
                B_hT.read(tp)
                BK[6].wrote(tp)
                BK[7].wrote(tp)
                yield
                us = i % 3
                t6 = R.op("dve", lambda e: e.tensor_scalar(out=qk_tm, in0=bank(6), scalar1=rstd[:, i:i + 1],
                                                           scalar2=None, op0=ALU.mult),
                          waits=[tp, t_rstd] + B_qk.wdeps(), mark=True)
                B_qk.wrote(t6)
                BK[6].read(t6)
                t7 = R.op("act", lambda e: e.activation(out=Vx3[:, i, 0:256], in_=bank(7)[:, 0:256], func=AF.Identity,
                                                        scale=rstd[:, i:i + 1]),
                          waits=[tp, t_rstd, t_ones], mark=True)
                t8 = R.op("act", lambda e: e.activation(out=ut[us], in_=bank(7)[:, 256:512], func=AF.Identity,
                                                        scale=rstd[:, i:i + 1]),
                          waits=[tp, t_rstd] + B_ut[us].wdeps(), mark=True)
                B_ut[us].wrote(t8)
                BK[7].read(t7, t8)
                qps = bank16(4)
                w = B_qk.rdeps() + BK[4].wdeps()
                for j in range(4):
                    tp = R.op("pe", lambda e, j=j: e.transpose(out=qps[:, j * 128:(j + 1) * 128],
                                                                in_=qk_tm[:, j * 128:(j + 1) * 128], identity=idb),
                              waits=w if j == 0 else (), mark=(j == 3))
                B_qk.read(tp)
                BK[4].wrote(tp)
                qs = i % 2
                t9 = R.op("dve", lambda e: e.tensor_copy(out=qT[qs], in_=qps[:, 0:256]),
                          waits=[tp] + B_qT[qs].wdeps(), mark=True)
                B_qT[qs].wrote(t9)
                t10 = R.op("dve", lambda e: e.tensor_copy(out=kT3[:, :, i * 128:(i + 1) * 128],
                                                          in_=qps[:, 256:512].rearrange("p (h t) -> p h t", h=2)),
                           waits=[tp], mark=True)
                BK[4].read(t9, t10)
                ret.append((t7, t10))
                yield
                pps = bank(5)
                w = B_ut[us].rdeps() + BK[5].wdeps() + SETUP_ALL
                if i > 0:
                    w += B_ut[(i - 1) % 3].rdeps()
                first = True
                for cc in range(2):
                    b0 = bandsb[:, 0:128] if i == 0 else bandsb[:, 128:256]
                    tp = R.op("pe", lambda e, cc=cc, b0=b0: e.matmul(pps[:, cc * 128:(cc + 1) * 128],
                                                                     lhsT=ut[us][:, cc * 128:(cc + 1) * 128], rhs=b0,
                                                                     start=True, stop=(i == 0)),
                              waits=w if first else (), mark=(i == 0 and cc == 1))
                    first = False
                    if i > 0:
                        up = ut[(i - 1) % 3]
                        tp = R.op("pe", lambda e, cc=cc, up=up: e.matmul(pps[:, cc * 128:(cc + 1) * 128],
                                                                         lhsT=up[:, cc * 128:(cc + 1) * 128],
                                                                         rhs=bandsb[:, 256:384], start=False, stop=True),
                                  mark=(cc == 1))
                B_ut[us].read(tp)
                if i > 0:
                    B_ut[(i - 1) % 3].read(tp)
                BK[5].wrote(tp)
                t11 = R.op("dve", lambda e: e.tensor_copy(out=pooledT, in_=pps[:, 0:256]),
                           waits=[tp] + B_pooled.wdeps(), mark=True)
                B_pooled.wrote(t11)
                BK[5].read(t11)
                w = [t11] + BK[5].wdeps()
                first = True
                for dc in range(2):
                    for cc in range(2):
                        tp = R.op("pe", lambda e, dc=dc, cc=cc: e.matmul(
                            pps[:, 256 + dc * 128:256 + (dc + 1) * 128],
                            lhsT=pw[:, cc * 256 + dc * 128:cc * 256 + (dc + 1) * 128],
                            rhs=pooledT[:, cc * 128:(cc + 1) * 128], start=(cc == 0), stop=(cc == 1)),
                            waits=w if first else (), mark=(dc == 1 and cc == 1))
                        first = False
                B_pooled.read(tp)
                BK[5].wrote(tp)
                sb = (i // TS) % 2
                tc = i % TS
                st3 = stage[sb].rearrange("p (a t) -> p a t", a=4)
                wst = B_stage[sb].wdeps() if tc == 0 else []
                for dc in range(2):
                    t12 = R.op("dve", lambda e, dc=dc: e.tensor_scalar(
                        out=st3[:, 2 + dc, tc * 128:(tc + 1) * 128], in0=pps[:, 256 + dc * 128:256 + (dc + 1) * 128],
                        scalar1=psc[:, dc:dc + 1], scalar2=None, op0=ALU.mult),
                        waits=[tp] + wst, mark=True)
                BK[5].read(t12)
                if tc == 0:
                    B_stage[sb].wrote(t12)
                else:
                    B_stage[sb].w.append(t12)
                return

            def attn_tile(i, t_kv, hooks):
                qs = i % 2
                groups = [(k0, min(2, i + 1 - k0)) for k0 in range(0, i + 1, 2)]
                pend = None

                def emit_qk(k0, n):
                    sbk = 2 + (st_ctr[0] % 2)
                    st_ctr[0] += 1
                    stp = bank(sbk)
                    w = B_qT[qs].rdeps() + BK[sbk].wdeps() + SETUP_ALL + list(t_kv)
                    first = True
                    for a in range(n):
                        kb = k0 + a
                        for h in range(2):
                            diag = (kb == i)
                            col = (a * 2 + h) * 128
                            tp = R.op("pe", lambda e, kb=kb, h=h, col=col, diag=diag: e.matmul(
                                stp[:, col:col + 128], lhsT=kT3[:, h, kb * 128:(kb + 1) * 128],
                                rhs=qT[qs][:, h * 128:(h + 1) * 128], start=True, stop=not diag),
                                waits=w if first else (), mark=(not diag and a == n - 1 and h == 1))
                            first = False
                            if diag:
                                tp = R.op("pe", lambda e, col=col: e.matmul(stp[:, col:col + 128], lhsT=idb, rhs=maskb,
                                                                            start=False, stop=True),
                                          mark=(h == 1))
                    BK[sbk].wrote(tp)
                    B_qT[qs].read(tp)
                    return (sbk, tp, k0, n)

                def emit_exp_pv(sbk, tqk, k0, n):
                    pb = pt_ctr[0] % 2
                    pt_ctr[0] += 1
                    te = R.op("act", lambda e: e.activation(out=PT[pb][:, 0:n * 256], in_=bank(sbk)[:, 0:n * 256],
                                                            func=AF.Exp, scale=SC_ATT),
                              waits=[tqk] + B_PT[pb].wdeps(), mark=True)
                    B_PT[pb].wrote(te)
                    BK[sbk].read(te)
                    w = [te, t_ones] + (BK[0].wdeps() + BK[1].wdeps() if k0 == 0 else [])
                    first = True
                    for a in range(n):
                        kb = k0 + a
                        for h in range(2):
                            col = (a * 2 + h) * 128
                            tp = R.op("pe", lambda e, kb=kb, h=h, col=col: e.matmul(
                                bank(h)[:, 0:257], lhsT=PT[pb][:, col:col + 128], rhs=Vx3[:, kb, 0:257],
                                start=(kb == 0), stop=(kb == i)),
                                waits=w if first else (), mark=(a == n - 1 and h == 1))
                            first = False
                    B_PT[pb].read(tp)
                    return tp

                tlast = None
                for gidx, (k0, n) in enumerate(groups):
                    cur = emit_qk(k0, n)
                    if pend is not None:
                        tlast = emit_exp_pv(*pend)
                    pend = cur
                    if hooks and gidx >= 1 and gidx % 2 == 1:
                        hooks.pop(0)()
                tlast = emit_exp_pv(*pend)
                while hooks:
                    hooks.pop(0)()
                BK[0].wrote(tlast)
                BK[1].wrote(tlast)
                w = [tlast, t_lam, t_sub] + B_fin.wdeps() + B_o1.wdeps() + B_osb.wdeps() + B_ob.wdeps()
                ta = R.op("dve", lambda e: e.reciprocal(out=fin[:, 0:1], in_=bank(0)[:, 256:257]), waits=w, mark=True)
                tb = R.op("dve", lambda e: e.reciprocal(out=fin[:, 1:2], in_=bank(1)[:, 256:257]), mark=True)
                tcx = R.op("dve", lambda e: e.tensor_tensor(out=fin[:, 2:3], in0=fin[:, 1:2], in1=neglam, op=ALU.mult),
                           waits=[tb], mark=True)
                td = R.op("dve", lambda e: e.tensor_scalar(out=o1, in0=bank(0)[:, 0:256], scalar1=fin[:, 0:1],
                                                           scalar2=None, op0=ALU.mult), waits=[ta], mark=True)
                te = R.op("dve", lambda e: e.scalar_tensor_tensor(out=osb, in0=bank(1)[:, 0:256], scalar=fin[:, 2:3],
                                                                  in1=o1, op0=ALU.mult, op1=ALU.add),
                          waits=[tcx, td], mark=True)
                BK[0].read(te)
                BK[1].read(te)
                tf = R.op("dve", lambda e: e.scalar_tensor_tensor(out=junkd[:, 0:256], in0=osb, scalar=1.0, in1=osb,
                                                                  op0=ALU.mult, op1=ALU.mult, accum_out=fin[:, 3:4]),
                          waits=[te] + B_junkd.wdeps(), mark=True)
                B_junkd.wrote(tf)
                tg = R.op("dve", lambda e: e.tensor_scalar(out=fin[:, 4:5], in0=fin[:, 3:4], scalar1=1.0 / 256,
                                                           scalar2=EPS, op0=ALU.mult, op1=ALU.add),
                          waits=[tf], mark=True)
                th = R.op("act", lambda e: e.activation(out=fin[:, 5:6], in_=fin[:, 4:5], func=AF.Ln),
                          waits=[tg], mark=True)
                ti = R.op("act", lambda e: e.activation(out=fin[:, 6:7], in_=fin[:, 5:6], func=AF.Exp, scale=-0.5),
                          waits=[th], mark=True)
                tj = R.op("dve", lambda e: e.scalar_tensor_tensor(out=ob, in0=osb, scalar=fin[:, 6:7], in1=sub_bc,
                                                                  op0=ALU.mult, op1=ALU.mult),
                          waits=[ti], mark=True)
                B_fin.wrote(tj)
                B_o1.wrote(tj)
                B_osb.wrote(tj)
                B_ob.wrote(tj)
                deferred.append(i)

            def finalize_pe(i):
                ops = bank16(6)
                w = B_ob.rdeps() + BK[6].wdeps()
                for dc in range(2):
                    tp = R.op("pe", lambda e, dc=dc: e.transpose(out=ops[:, dc * 128:(dc + 1) * 128],
                                                                  in_=ob[:, dc * 128:(dc + 1) * 128], identity=idb),
                              waits=w if dc == 0 else (), mark=(dc == 1))
                B_ob.read(tp)
                BK[6].wrote(tp)
                sb = (i // TS) % 2
                tc = i % TS
                st3 = stage[sb].rearrange("p (a t) -> p a t", a=4)
                tk = R.op("dve", lambda e: e.tensor_copy(out=st3[:, 0:2, tc * 128:(tc + 1) * 128],
                                                         in_=ops[:, 0:256].rearrange("p (a t) -> p a t", a=2)),
                          waits=[tp], mark=True)
                BK[6].read(tk)
                B_stage[sb].w.append(tk)
                if tc == TS - 1:
                    t0 = (i // TS) * TS * 128
                    pj, pc = t0 // PIECE, t0 % PIECE
                    tdma = R.op("pool", lambda e: e.dma_start(
                        out=xch_in[pj].ap().rearrange("(a p) t -> p a t", p=128)[:, :, pc:pc + TS * 128], in_=st3),
                        waits=B_stage[sb].rdeps(), sem=SST[sb], inc=16)
                    B_stage[sb].read(tdma)
                    stage_done.append(tdma)
                    if (t0 + TS * 128) % PIECE == 0:
                        for t_ in stage_done[-2:]:
                            R.wait("pool", t_)
                        cc_toks.append(R.op("pool", lambda e: e.collective_compute(
                            "AllGather", ALU.bypass, replica_groups=[[0, 1, 2, 3], [4, 5, 6, 7]],
                            ins=[xch_in[pj].ap()], outs=[xch_out.ap()[pj]]), sem=SCC, inc=1))

            def run_all(g):
                for _ in g:
                    pass

            ret0 = []
            run_all(proj_tile(0, ret0))
            kv_next = ret0
            for i in range(NT):
                t_kv = kv_next[0]
                hooks = []
                for d_ in list(deferred):
                    hooks.append(lambda d_=d_: finalize_pe(d_))
                deferred.clear()
                if i + 1 < NT:
                    kv_next = []
                    g_ = proj_tile(i + 1, kv_next)
                    for _ in range(4):
                        hooks.append(lambda g_=g_: next(g_, None))
                attn_tile(i, t_kv, hooks)
            while deferred:
                finalize_pe(deferred.pop(0))

            if STOP_AT < 3:
                raise _Stop()
            assert len(cc_toks) == NP
            T_cc = cc_toks[-1]
            for eng in ("pe", "act", "dve", "pool", "sp"):
                R.wait(eng, T_cc)

            if STOP_AT < 4:
                raise _Stop()
            A.off = common_end
            xres = A.alloc(TG * 2048, F32)
            xres3 = xres.rearrange("p (t f) -> p t f", f=2048)
            gb_x = A.alloc(2048, F32)
            gb_f = A.alloc(2048, F32)
            gb_o = A.alloc(2048, F32)
            bufA = A.alloc(16 * G)
            bufB = A.alloc(16 * G)
            NFH = 24
            bufC = A.alloc(NFH * G)
            kTm = A.alloc(16 * MEM)
            vm = A.alloc(2 * 2048)
            ring = [A.alloc(8192) for _ in range(NW)]
            hs2 = A.alloc(2048)
            rl = hs2.bitcast(F32)[:, 0:G]
            PT2 = [A.alloc(G) for _ in range(2)]
            sg = [A.alloc(G) for _ in range(2)]
            st2 = A.alloc(64, F32)
            bufA3 = bufA.rearrange("p (c t) -> p c t", t=G)
            bufB3 = bufB.rearrange("p (c t) -> p c t", t=G)
            bufC3 = bufC.rearrange("p (c t) -> p c t", t=G)
            kTm3 = kTm.rearrange("p (c m) -> p c m", m=MEM)
            vm3 = vm.rearrange("p (c f) -> p c f", f=2048)

            B_xres = [Buf() for _ in range(TG)]
            B_A = Buf()
            B_B = Buf()
            B_C = Buf()
            B_ring = [Buf() for _ in range(NW)]
            B_hs2 = Buf()
            B_PT2 = [Buf(), Buf()]
            B_rl = Buf()
            B_sg = [Buf(), Buf()]
            for b in BK:
                b.w = []
                b.r = []
            bk_ctr = [0]
            ring_ctr = [0]
            st2_ctr = [0]

            def next_bank():
                b = bk_ctr[0] % 8
                bk_ctr[0] += 1
                return b

            def load_w(src_ap):
                k = ring_ctr[0] % NW
                ring_ctr[0] += 1
                L = src_ap.shape[1]
                t = R.op("sp", lambda e: e.dma_start(out=ring[k][:, 0:L], in_=src_ap),
                         waits=B_ring[k].wdeps() + [T_wcast], sem=SR[k], inc=16)
                B_ring[k].wrote(t)
                return k, t

            ld = R.op("sp", lambda e: e.dma_start(out=gb_x, in_=bass.AP(g_xat, 0, [[0, 128], [1, 2048]])),
                      sem=SMEM, inc=16)
            ld = R.op("sp", lambda e: e.dma_start(out=gb_f, in_=bass.AP(g_ffn, 0, [[0, 128], [1, 2048]])),
                      sem=SMEM, inc=16)
            ld = R.op("sp", lambda e: e.dma_start(out=gb_o, in_=bass.AP(g_mem, 0, [[0, 128], [1, 2048]])),
                      sem=SMEM, inc=16)
            T_gb = ld

            def norm_transpose(src_tile_ap, gb, dst3, tcol, src_buf, dst_buf, ncols=128, extra_w=()):
                k = st2_ctr[0] % 16
                st2_ctr[0] += 1
                c0 = k * 4
                t1 = R.op("act", lambda e: e.activation(out=junk, in_=src_tile_ap, func=AF.Square,
                                                        accum_out=st2[:, c0:c0 + 1]),
                          waits=src_buf.rdeps() + list(extra_w) + B_junk.wdeps(), mark=True)
                B_junk.wrote(t1)
                t2 = R.op("dve", lambda e: e.tensor_scalar(out=st2[:, c0 + 1:c0 + 2], in0=st2[:, c0:c0 + 1],
                                                           scalar1=1.0 / D, scalar2=EPS, op0=ALU.mult, op1=ALU.add),
                          waits=[t1], mark=True)
                t3a = R.op("act", lambda e: e.activation(out=st2[:, c0 + 3:c0 + 4], in_=st2[:, c0 + 1:c0 + 2],
                                                         func=AF.Ln), waits=[t2], mark=True)
                t3 = R.op("act", lambda e: e.activation(out=st2[:, c0 + 2:c0 + 3], in_=st2[:, c0 + 3:c0 + 4],
                                                        func=AF.Exp, scale=-0.5), waits=[t3a], mark=True)
                t4 = R.op("dve", lambda e: e.scalar_tensor_tensor(out=hs2, in0=src_tile_ap, scalar=st2[:, c0 + 2:c0 + 3],
                                                                  in1=gb, op0=ALU.mult, op1=ALU.mult),
                          waits=[t3, T_gb] + src_buf.rdeps() + B_hs2.wdeps(), mark=True)
                B_hs2.wrote(t4)
                src_buf.read(t1, t4)
                toks = []
                for half in range(2):
                    b = next_bank()
                    hp = bank16(b)
                    w = [t4] + BK[b].wdeps()
                    for c8 in range(8):
                        c = half * 8 + c8
                        tp = R.op("pe", lambda e, c=c, c8=c8, hp=hp: e.transpose(
                            out=hp[:, c8 * 128:(c8 + 1) * 128], in_=hs2[:, c * 128:(c + 1) * 128], identity=idb),
                            waits=w if c8 == 0 else (), mark=(c8 == 7))
                    BK[b].wrote(tp)
                    eng = "act" if half == 0 else "dve"
                    dst = dst3[:, half * 8:(half + 1) * 8, tcol * 128:(tcol + 1) * 128]
                    src = hp[:, 0:1024].rearrange("p (c t) -> p c t", c=8)
                    if eng == "act":
                        te = R.op("act", lambda e, dst=dst, src=src: e.activation(out=dst, in_=src, func=AF.Copy),
                                  waits=[tp] + dst_buf.wdeps(), mark=True)
                    else:
                        te = R.op("dve", lambda e, dst=dst, src=src: e.tensor_copy(out=dst, in_=src),
                                  waits=[tp] + dst_buf.wdeps(), mark=True)
                    BK[b].read(te)
                    toks.append(te)
                B_hs2.read(tp)
                return toks

            B_mem = Buf()
            mT3 = bufA.rearrange("p (c t) -> p c t", t=G)
            mtoks = []
            for mt in range(2):
                sl = mt
                tl = R.op("sp", lambda e, mt=mt: e.dma_start(out=xres3[:, mt, :], in_=mem_in.ap()[mt * 128:(mt + 1) * 128, :]),
                          sem=SY[mt], inc=16)
                B_xres[mt].wrote(tl)
                mtoks += norm_transpose(xres3[:, mt, :], gb_o, mT3, mt, B_xres[mt], B_A)
            B_A.wrote(*mtoks)
            for n in range(4):
                k, tw = load_w(s_wkv.ap()[n])
                wr = ring[k].rearrange("p (c n) -> p c n", n=512)
                for j in range(4):
                    b = next_bank()
                    w = [tw] + B_A.rdeps() + BK[b].wdeps()
                    for c in range(16):
                        tp = R.op("pe", lambda e, c=c, j=j, b=b, wr=wr: e.matmul(
                            bank(b)[:, 0:MEM], lhsT=wr[:, c, j * 128:(j + 1) * 128], rhs=mT3[:, c, 0:MEM],
                            start=(c == 0), stop=(c == 15)), waits=w if c == 0 else (), mark=(c == 15))
                    BK[b].wrote(tp)
                    te = R.op("act", lambda e, n=n, j=j, b=b: e.activation(out=kTm3[:, n * 4 + j, :],
                                                                            in_=bank(b)[:, 0:MEM], func=AF.Copy),
                              waits=[tp], mark=True)
                    BK[b].read(te)
                B_ring[k].read(tp)
            T_kTm = te
            for n in range(4):
                k, tw = load_w(s_wkv.ap()[4 + n])
                wr = ring[k].rearrange("p (c n) -> p c n", n=512)
                for mc in range(2):
                    b = next_bank()
                    w = [tw] + B_A.rdeps() + BK[b].wdeps()
                    for c in range(16):
                        tp = R.op("pe", lambda e, c=c, mc=mc, b=b, wr=wr: e.matmul(
                            bank(b), lhsT=mT3[:, c, mc * 128:(mc + 1) * 128], rhs=wr[:, c, :],
                            start=(c == 0), stop=(c == 15)), waits=w if c == 0 else (), mark=(c == 15))
                    BK[b].wrote(tp)
                    te = R.op("dve", lambda e, n=n, mc=mc, b=b: e.tensor_copy(out=vm3[:, mc, n * 512:(n + 1) * 512],
                                                                               in_=bank(b)), waits=[tp], mark=True)
                    BK[b].read(te)
                B_ring[k].read(tp)
            T_vm = te
            B_A.read(tp)
            T_gbfin = R.op("sp", lambda e: e.dma_start(out=gb_o, in_=bass.AP(g_fin, 0, [[0, 128], [1, 2048]])),
                           waits=[T_vm], sem=SMEM, inc=16)
            final_stores = []

            if STOP_AT < 5:
                raise _Stop()
            core = [None]
            SC_X = 512.0 ** -0.5

            def proj_tokmajor(src3, src_buf, wsrc, resid_add):
                for n in range(4):
                    k, tw = load_w(wsrc.ap()[n])
                    wr = ring[k].rearrange("p (c n) -> p c n", n=512)
                    for t in range(TG):
                        b = next_bank()
                        w = [tw] + src_buf.rdeps() + BK[b].wdeps()
                        for c in range(16):
                            tp = R.op("pe", lambda e, c=c, t=t, b=b, wr=wr: e.matmul(
                                bank(b), lhsT=src3[:, c, t * 128:(t + 1) * 128], rhs=wr[:, c, :],
                                start=(c == 0), stop=(c == 15)), waits=w if c == 0 else (), mark=(c == 15))
                        BK[b].wrote(tp)
                        te = R.op("dve", lambda e, t=t, n=n, b=b: e.tensor_tensor(
                            out=xres3[:, t, n * 512:(n + 1) * 512], in0=bank(b),
                            in1=xres3[:, t, n * 512:(n + 1) * 512], op=ALU.add),
                            waits=[tp] + B_xres[t].wdeps(), mark=True)
                        BK[b].read(te)
                        B_xres[t].wrote(te)
                    B_ring[k].read(tp)
                src_buf.read(tp)

            for gi in range(NG):
                tok0 = gi * G
                for t in range(TG):
                    tl = R.op("sp", lambda e, t=t, tok0=tok0: e.dma_start(out=xres3[:, t, :],
                                                               in_=x_sl.ap()[tok0 + t * 128:tok0 + (t + 1) * 128, :]),
                              waits=B_xres[t].wdeps(), sem=SXR, inc=16)
                for t in range(TG):
                    B_xres[t].wrote(tl)
                def ld_ot(e, tok0=tok0):
                    if core[0] is None:
                        core[0] = e.partition_id()
                    rank = core[0] % 4
                    pidx = (rank * (SL // PIECE) + tok0 // PIECE) * 16
                    pcol = tok0 % PIECE
                    src = xch_out.ap().rearrange("j (c p) t -> p (j c) t", p=128)[:, bass.ds(pidx, 16), pcol:pcol + G]
                    return e.dma_start(out=bufB3, in_=src)
                tl = R.op("pool", ld_ot, waits=B_B.wdeps() + [T_cc], sem=SOT, inc=16)
                B_B.wrote(tl)
                proj_tokmajor(bufB3, B_B, s_wo, True)
                toks = []
                for t in range(TG):
                    toks += norm_transpose(xres3[:, t, :], gb_x, bufA3, t, B_xres[t], B_A)
                B_A.wrote(*toks)
                first_d = True
                for n in range(4):
                    k, tw = load_w(s_wq.ap()[n])
                    wr = ring[k].rearrange("p (c n) -> p c n", n=512)
                    for j in range(4):
                        b = next_bank()
                        w = [tw] + B_A.rdeps() + BK[b].wdeps()
                        for c in range(16):
                            tp = R.op("pe", lambda e, c=c, j=j, b=b, wr=wr: e.matmul(
                                bank(b)[:, 0:G], lhsT=wr[:, c, j * 128:(j + 1) * 128], rhs=bufA3[:, c, :],
                                start=(c == 0), stop=(c == 15)), waits=w if c == 0 else (), mark=(c == 15))
                        BK[b].wrote(tp)
                        te = R.op("act", lambda e, n=n, j=j, b=b: e.activation(out=bufB3[:, n * 4 + j, :],
                                                                                in_=bank(b)[:, 0:G], func=AF.Copy),
                                  waits=[tp] + (B_B.wdeps() if first_d else []), mark=True)
                        first_d = False
                        BK[b].read(te)
                    B_ring[k].read(tp)
                B_A.read(tp)
                B_B.wrote(te)
                first_e = True
                for hh in range(4):
                    pts = []
                    for mc in range(2):
                        b = next_bank()
                        w = B_B.rdeps() + BK[b].wdeps() + [T_kTm]
                        for dd in range(4):
                            tp = R.op("pe", lambda e, dd=dd, mc=mc, b=b, hh=hh: e.matmul(
                                bank(b)[:, 0:G], lhsT=kTm3[:, hh * 4 + dd, mc * 128:(mc + 1) * 128],
                                rhs=bufB3[:, hh * 4 + dd, :], start=(dd == 0), stop=(dd == 3)),
                                waits=w if dd == 0 else (), mark=(dd == 3))
                        BK[b].wrote(tp)
                        te = R.op("act", lambda e, mc=mc, b=b: e.activation(out=PT2[mc], in_=bank(b)[:, 0:G],
                                                                             func=AF.Exp, scale=SC_X),
                                  waits=[tp] + B_PT2[mc].wdeps(), mark=True)
                        BK[b].read(te)
                        B_PT2[mc].wrote(te)
                        pts.append(te)
                    b = next_bank()
                    w = pts + BK[b].wdeps()
                    for mc in range(2):
                        tp = R.op("pe", lambda e, mc=mc, b=b: e.matmul(bank(b)[:, 0:G], lhsT=onesb, rhs=PT2[mc],
                                                                       start=(mc == 0), stop=(mc == 1)),
                                  waits=w if mc == 0 else (), mark=(mc == 1))
                    BK[b].wrote(tp)
                    trl = R.op("dve", lambda e, b=b: e.reciprocal(out=rl, in_=bank(b)[:, 0:G]),
                               waits=[tp] + B_rl.wdeps(), mark=True)
                    BK[b].read(trl)
                    B_rl.wrote(trl)
                    for dv in range(4):
                        b = next_bank()
                        w = pts + BK[b].wdeps() + [T_vm]
                        for mc in range(2):
                            tp = R.op("pe", lambda e, mc=mc, b=b, dv=dv, hh=hh: e.matmul(
                                bank(b)[:, 0:G], lhsT=vm3[:, mc, (hh * 4 + dv) * 128:(hh * 4 + dv + 1) * 128],
                                rhs=PT2[mc], start=(mc == 0), stop=(mc == 1)),
                                waits=w if mc == 0 else (), mark=(mc == 1))
                        BK[b].wrote(tp)
                        te = R.op("dve", lambda e, b=b, dv=dv, hh=hh: e.tensor_tensor(
                            out=bufA3[:, hh * 4 + dv, :], in0=bank(b)[:, 0:G], in1=rl, op=ALU.mult),
                            waits=[tp, trl] + (B_A.wdeps() if first_e else []), mark=True)
                        first_e = False
                        BK[b].read(te)
                    B_PT2[0].read(tp)
                    B_PT2[1].read(tp)
                    B_rl.read(te)
                B_B.read(tp)
                B_A.wrote(te)
                proj_tokmajor(bufA3, B_A, s_wox, True)
                toks = []
                for t in range(TG):
                    toks += norm_transpose(xres3[:, t, :], gb_f, bufA3, t, B_xres[t], B_A)
                B_A.wrote(*toks)
                f0 = 0
                for nf in (NFH, KD - NFH):
                    first_h = True
                    for wc in range(nf // 4):
                        n = (f0 // 4) + wc
                        kg, twg = load_w(s_wg.ap()[n])
                        ku, twu = load_w(s_wu.ap()[n])
                        wg = ring[kg].rearrange("p (c n) -> p c n", n=512)
                        wu = ring[ku].rearrange("p (c n) -> p c n", n=512)
                        for j in range(4):
                            fl = wc * 4 + j
                            bg = next_bank()
                            w = [twg] + B_A.rdeps() + BK[bg].wdeps()
                            for c in range(16):
                                tpg = R.op("pe", lambda e, c=c, j=j, b=bg, wr=wg: e.matmul(
                                    bank(b)[:, 0:G], lhsT=wr[:, c, j * 128:(j + 1) * 128], rhs=bufA3[:, c, :],
                                    start=(c == 0), stop=(c == 15)), waits=w if c == 0 else (), mark=(c == 15))
                            BK[bg].wrote(tpg)
                            bu = next_bank()
                            w = [twu] + BK[bu].wdeps()
                            for c in range(16):
                                tpu = R.op("pe", lambda e, c=c, j=j, b=bu, wr=wu: e.matmul(
                                    bank(b)[:, 0:G], lhsT=wr[:, c, j * 128:(j + 1) * 128], rhs=bufA3[:, c, :],
                                    start=(c == 0), stop=(c == 15)), waits=w if c == 0 else (), mark=(c == 15))
                            BK[bu].wrote(tpu)
                            si = fl % 2
                            ts_ = R.op("act", lambda e, b=bg, si=si: e.activation(out=sg[si], in_=bank(b)[:, 0:G],
                                                                                   func=AF.Silu),
                                       waits=[tpg] + B_sg[si].wdeps(), mark=True)
                            BK[bg].read(ts_)
                            B_sg[si].wrote(ts_)
                            tm = R.op("dve", lambda e, b=bu, si=si, fl=fl: e.tensor_tensor(
                                out=bufC3[:, fl, :], in0=bank(b)[:, 0:G], in1=sg[si], op=ALU.mult),
                                waits=[tpu, ts_] + (B_C.wdeps() if first_h else []), mark=True)
                            first_h = False
                            BK[bu].read(tm)
                            B_sg[si].read(tm)
                        B_ring[kg].read(tpg)
                        B_ring[ku].read(tpu)
                    B_A.read(tpu)
                    B_C.wrote(tm)
                    subs = []
                    kk = 0
                    while kk < nf:
                        ln = min(12, nf - kk)
                        subs.append((kk, ln))
                        kk += ln
                    for n in range(4):
                        banks = [next_bank() for _ in range(TG)]
                        for si_, (k0, ln) in enumerate(subs):
                            src = s_wd.ap()[n][:, (f0 + k0) * 512:(f0 + k0 + ln) * 512]
                            k, tw = load_w(src)
                            wr = ring[k].rearrange("p (c n) -> p c n", n=512)
                            for t in range(TG):
                                b = banks[t]
                                w = [tw] + B_C.rdeps() + (BK[b].wdeps() if si_ == 0 else [])
                                for c in range(ln):
                                    st_f = (si_ == 0 and c == 0)
                                    sp_f = (si_ == len(subs) - 1 and c == ln - 1)
                                    tp = R.op("pe", lambda e, c=c, t=t, b=b, wr=wr, k0=k0, st_f=st_f, sp_f=sp_f: e.matmul(
                                        bank(b), lhsT=bufC3[:, k0 + c, t * 128:(t + 1) * 128], rhs=wr[:, c, :],
                                        start=st_f, stop=sp_f),
                                        waits=w if c == 0 else (), mark=(c == ln - 1))
                                if si_ == len(subs) - 1:
                                    BK[b].wrote(tp)
                            B_ring[k].read(tp)
                        for t in range(TG):
                            b = banks[t]
                            te = R.op("dve", lambda e, t=t, n=n, b=b: e.tensor_tensor(
                                out=xres3[:, t, n * 512:(n + 1) * 512], in0=bank(b),
                                in1=xres3[:, t, n * 512:(n + 1) * 512], op=ALU.add),
                                waits=BK[b].rdeps() + B_xres[t].wdeps(), mark=True)
                            BK[b].read(te)
                            B_xres[t].wrote(te)
                    B_C.read(tp)
                    f0 += nf
                for t in range(TG):
                    k = st2_ctr[0] % 16
                    st2_ctr[0] += 1
                    c0 = k * 4
                    ys = t % 2
                    src = xres3[:, t, :]
                    t1 = R.op("act", lambda e, src=src, c0=c0: e.activation(out=junk, in_=src, func=AF.Square,
                                                                            accum_out=st2[:, c0:c0 + 1]),
                              waits=B_xres[t].rdeps() + B_junk.wdeps(), mark=True)
                    B_junk.wrote(t1)
                    t2 = R.op("dve", lambda e, c0=c0: e.tensor_scalar(out=st2[:, c0 + 1:c0 + 2], in0=st2[:, c0:c0 + 1],
                                                                      scalar1=1.0 / D, scalar2=EPS, op0=ALU.mult,
                                                                      op1=ALU.add), waits=[t1], mark=True)
                    t3a = R.op("act", lambda e, c0=c0: e.activation(out=st2[:, c0 + 3:c0 + 4],
                                                                    in_=st2[:, c0 + 1:c0 + 2], func=AF.Ln),
                               waits=[t2], mark=True)
                    t3 = R.op("act", lambda e, c0=c0: e.activation(out=st2[:, c0 + 2:c0 + 3],
                                                                   in_=st2[:, c0 + 3:c0 + 4], func=AF.Exp,
                                                                   scale=-0.5), waits=[t3a], mark=True)
                    t4 = R.op("dve", lambda e, src=src, c0=c0: e.scalar_tensor_tensor(
                        out=src, in0=src, scalar=st2[:, c0 + 2:c0 + 3], in1=gb_o, op0=ALU.mult, op1=ALU.mult),
                        waits=[t3, T_gbfin] + B_xres[t].wdeps(), mark=True)
                    B_xres[t].wrote(t4)
                    tst = R.op("sp", lambda e, t=t, src=src, tok0=tok0: e.dma_start(
                        out=y_out.ap()[tok0 + t * 128:tok0 + (t + 1) * 128, :], in_=src),
                        waits=[t4], sem=SY[t], inc=16)
                    B_xres[t].read(tst)
                    final_stores.append(tst)

            for t in final_stores[-TG:]:
                R.wait("sp", t)

        except _Stop:
            pass

        @block.tensor
        def _(e):
            for f in R.q["pe"]:
                f(e)

        @block.scalar
        def _(e):
            for f in R.q["act"]:
                f(e)

        @block.vector
        def _(e):
            for f in R.q["dve"]:
                f(e)

        @block.gpsimd
        def _(e):
            for f in R.q["pool"]:
                f(e)

        @block.sync
        def _(e):
            for f in R.q["sp"]:
                f(e)

    return nc


POOL_WINDOWS = (2, 4, 8, 16)


def _bands(w):
    j = np.arange(128)[:, None]
    t = np.arange(128)[None, :]
    inwin = (j <= t) & (j > t - w)
    cnt_first = np.minimum(t + 1, w).astype(np.float32)
    eye = (j == t).astype(np.float32)
    b0f = inwin / cnt_first - eye
    b0 = inwin / np.float32(w) - eye
    b1 = ((j - 128) > (t - w)).astype(np.float32) / np.float32(w)
    return np.ascontiguousarray(np.concatenate([b0f, b0, b1], axis=1).astype(np.float32))


_CACHE = {}


def kernel(x, mem, norm_mix, w_in, lambda_q1, lambda_k1, lambda_q2, lambda_k2, subln, pool_w, pool_scale,
           w_o, norm_xattn, norm_mem, wq_x, wkv_x, wo_x, norm_ffn, w_gate, w_up, w_down, norm_final):
    f = lambda a: np.ascontiguousarray(np.asarray(a, dtype=np.float32))
    x = f(x)
    mem = f(mem)
    B, S, _ = x.shape
    SL = S // 4
    if S not in _CACHE:
        _CACHE[S] = build_program(S)
    nc = _CACHE[S]
    w_in0 = f(w_in)[0]
    w_o0 = f(w_o)[0]
    perm = np.concatenate([np.concatenate([np.arange(r * 256, (r + 1) * 256),
                                           1024 + np.arange(r * 256, (r + 1) * 256)]) for r in range(4)])
    w_o_p = np.ascontiguousarray(w_o0[perm])
    kk = np.arange(128)[:, None]
    qq = np.arange(128)[None, :]
    mask = np.where(kk > qq, np.float32(NEG), np.float32(0.0)).astype(np.float32)
    ident = np.eye(128, dtype=np.float32)
    lam = np.concatenate([f(lambda_q1)[0], f(lambda_k1)[0], f(lambda_q2)[0], f(lambda_k2)[0]])
    shared = {
        "gT_mix": np.ascontiguousarray(f(norm_mix)[0].reshape(16, 128).T),
        "g_xat": f(norm_xattn)[0], "g_mem": f(norm_mem)[0], "g_ffn": f(norm_ffn)[0], "g_fin": f(norm_final),
        "lam_in": np.ascontiguousarray(lam), "subln_in": f(subln)[0],
        "ident_in": ident, "mask_in": mask,
    }
    wfull = {"w_o_p": w_o_p, "wq_x": f(wq_x)[0], "wkv_x": f(wkv_x)[0], "wo_x": f(wo_x)[0],
             "w_gate": f(w_gate)[0], "w_up": f(w_up)[0], "w_down": f(w_down)[0]}
    in_maps = []
    for c in range(8):
        b, r = divmod(c, 4)
        cols = np.concatenate([k * 1024 + np.arange(r * 256, (r + 1) * 256) for k in range(4)])
        m = dict(shared)
        m["x_sl"] = np.ascontiguousarray(x[b, r * SL:(r + 1) * SL])
        m["mem_b"] = mem[b]
        m["w_in_c"] = np.ascontiguousarray(w_in0[:, cols])
        m["poolw_in"] = np.ascontiguousarray(f(pool_w)[0, r])
        m["psc_in"] = np.ascontiguousarray(f(pool_scale)[0, r * 256:(r + 1) * 256].reshape(2, 128).T)
        m["bands_in"] = _bands(POOL_WINDOWS[r])
        m["x_full"] = x[b]
        m.update(wfull)
        in_maps.append(m)
    res = run_bass_kernel_spmd(nc, in_maps, core_ids=list(range(8)))
    out = np.empty((B, S, D), dtype=np.float32)
    for c in range(8):
        b, r = divmod(c, 4)
        out[b, r * SL:(r + 1) * SL] = np.asarray(res.results[c]["y"], dtype=np.float32)
    return out
```
